# Optimizing a Trainium2 kernel written in Bass

```python
import math
import jax
import jax.numpy as jnp
from jax import lax
import numpy as np

D_MODEL = 1024
BATCH = 8
SEQ = 4096
DEPTH = 1

GRID_W = 64
CTX_LEN = 256
MIX_WIDTH = D_MODEL
FOURIER_HEAD_DIM = 64
FOURIER_WIDTH = MIX_WIDTH // 4
FOURIER_HEADS = FOURIER_WIDTH // FOURIER_HEAD_DIM
SSD_WIDTH = MIX_WIDTH - FOURIER_WIDTH
SSD_HEAD_DIM = 64
SSD_HEADS = SSD_WIDTH // SSD_HEAD_DIM
SSD_GROUPS = 4
SSD_STATE = 128
GROUP_STATE = SSD_GROUPS * SSD_STATE
CONV_WIDTH = 3
CHUNK = 128
D_FF = 2816
N_MOD = 9
MACARON_WEIGHT = 0.5
EPS = 1e-6
CONV_DIM = GROUP_STATE + SSD_WIDTH + GROUP_STATE
PROJ_WIDTH = FOURIER_WIDTH + SSD_WIDTH + CONV_DIM + 2 * SSD_HEADS

kernel_name = 'hybrid_fourier_ssd_dit_layer'


def rms_norm(x, g):
    x32 = x.astype(jnp.float32)
    y = x32 * lax.rsqrt(jnp.mean(x32 * x32, axis=-1, keepdims=True) + EPS)
    return (y * g.astype(jnp.float32)).astype(x.dtype)


def modulate(h, shift, scale):
    return h * (1 + scale[:, None, :]) + shift[:, None, :]


def macaron_ffn(x, mod3, norm_pre, norm_post, w_gate, w_up, w_down):
    h = modulate(rms_norm(x, norm_pre), mod3[:, 0], mod3[:, 1])
    y = (jax.nn.silu(h @ w_gate) * (h @ w_up)) @ w_down
    return x + MACARON_WEIGHT * mod3[:, 2][:, None, :] * rms_norm(y, norm_post)


def centred_dwconv_silu(u, w, b):
    out = lax.conv_general_dilated(u, w[:, None, :], (1,), 'SAME',
                                   dimension_numbers=('NWC', 'WIO', 'NWC'),
                                   feature_group_count=u.shape[-1])
    return jax.nn.silu(out + b)


def softplus_dt(dt_raw, dt_bias):
    b, L = dt_raw.shape[:2]
    return jax.nn.softplus(dt_raw.astype(jnp.float32).reshape(b, L, 2, SSD_HEADS)
                           + dt_bias.astype(jnp.float32))


def flip(t):
    return jnp.flip(t, axis=1)


def fourier_mix(u, fourier_w):
    b, L, _ = u.shape
    uh = u.reshape(b, L, FOURIER_HEADS, FOURIER_HEAD_DIM).astype(jnp.float32)
    f = jnp.fft.fft2(uh, axes=(1, 3), norm='ortho').real.astype(u.dtype)
    return jnp.einsum('blhd,hde->blhe', f, fourier_w).reshape(b, L, FOURIER_WIDTH)


def ssd_chunked(xs, dt, a, bm, cm, h0):
    b, L = xs.shape[:2]
    nc = L // CHUNK
    R = SSD_HEADS // SSD_GROUPS
    xdt = (xs * dt[..., None]).reshape(b, nc, CHUNK, SSD_GROUPS, R, SSD_HEAD_DIM)
    a_cs = jnp.cumsum(jnp.transpose((dt * a).reshape(b, nc, CHUNK, SSD_GROUPS, R), (0, 1, 3, 4, 2)), axis=-1)
    bc = bm.reshape(b, nc, CHUNK, SSD_GROUPS, SSD_STATE)
    cc = cm.reshape(b, nc, CHUNK, SSD_GROUPS, SSD_STATE)
    lower = jnp.tril(jnp.ones((CHUNK, CHUNK), dtype=bool))
    seg = jnp.exp(jnp.where(lower, a_cs[..., :, None] - a_cs[..., None, :], -jnp.inf))
    cb = jnp.einsum('bclgn,bcsgn->bcgls', cc, bc)
    y_diag = jnp.einsum('bcgls,bcgrls,bcsgrp->bclgrp', cb, seg, xdt)
    states = jnp.einsum('bcsgn,bcgrs,bcsgrp->bcgrpn', bc, jnp.exp(a_cs[..., -1:] - a_cs), xdt)

    def step(h, inp):
        s, d = inp
        return h * d[..., None, None] + s, h

    h_last, h_prev = lax.scan(step, h0.reshape(b, SSD_GROUPS, R, SSD_HEAD_DIM, SSD_STATE),
                              (jnp.moveaxis(states, 1, 0), jnp.moveaxis(jnp.exp(a_cs[..., -1]), 1, 0)))
    y_off = jnp.einsum('bclgn,cbgrpn,bcgrl->bclgrp', cc, h_prev, jnp.exp(a_cs))
    y = (y_diag + y_off).reshape(b, L, SSD_HEADS, SSD_HEAD_DIM)
    return y, h_last.reshape(b, SSD_HEADS, SSD_HEAD_DIM, SSD_STATE)


def ssd_final_state(xs, dt, a, bm):
    b, L = xs.shape[:2]
    R = SSD_HEADS // SSD_GROUPS
    cs = jnp.cumsum(dt * a, axis=1)
    xdt = (xs * (dt * jnp.exp(cs[:, -1:] - cs))[..., None]).reshape(b, L, SSD_GROUPS, R, SSD_HEAD_DIM)
    return jnp.einsum('blgn,blgrp->bgrpn', bm, xdt).reshape(b, SSD_HEADS, SSD_HEAD_DIM, SSD_STATE)


def mix_sublayer(x, mod3, h0_f, h0_b, norm_pre, norm_post, w_in, fourier_w, conv_w, conv_b,
                 dt_bias, a_log, d_skip, ssd_norm, w_out):
    b, L, _ = x.shape
    h = modulate(rms_norm(x, norm_pre), mod3[:, 0], mod3[:, 1])
    u_f, z, xbc, dt_raw = jnp.split(h @ w_in, [FOURIER_WIDTH, FOURIER_WIDTH + SSD_WIDTH,
                                               FOURIER_WIDTH + SSD_WIDTH + CONV_DIM], axis=-1)
    xbc = centred_dwconv_silu(xbc, conv_w, conv_b)
    cm, xs, bm = jnp.split(xbc, [GROUP_STATE, GROUP_STATE + SSD_WIDTH], axis=-1)
    xs = xs.reshape(b, L, SSD_HEADS, SSD_HEAD_DIM)
    bm = bm.reshape(b, L, SSD_GROUPS, SSD_STATE)
    cm = cm.reshape(b, L, SSD_GROUPS, SSD_STATE)
    dt = softplus_dt(dt_raw, dt_bias)
    a = -jnp.exp(a_log.astype(jnp.float32))
    y_f, h_f = ssd_chunked(xs, dt[:, :, 0], a[0], bm, cm, h0_f)
    y_b, h_b = ssd_chunked(flip(xs), flip(dt[:, :, 1]), a[1], flip(bm), flip(cm), h0_b)
    y = y_f + flip(y_b) + d_skip.astype(jnp.float32)[:, None] * xs
    y = y.reshape(b, L, SSD_WIDTH).astype(x.dtype) * jax.nn.silu(z)
    y = rms_norm(y.reshape(b, L, SSD_GROUPS, SSD_WIDTH // SSD_GROUPS),
                 ssd_norm.reshape(SSD_GROUPS, SSD_WIDTH // SSD_GROUPS)).reshape(b, L, SSD_WIDTH)
    out = jnp.concatenate([fourier_mix(u_f, fourier_w), y], axis=-1) @ w_out
    return x + mod3[:, 2][:, None, :] * rms_norm(out, norm_post), h_f, h_b


def context_states(ctx, mod3, norm_pre, w_in, conv_w, conv_b, dt_bias, a_log):
    b, L, _ = ctx.shape
    h = modulate(rms_norm(ctx, norm_pre), mod3[:, 0], mod3[:, 1])
    xb, dt_raw = jnp.split(h @ w_in[:, FOURIER_WIDTH + SSD_WIDTH + GROUP_STATE:],
                           [SSD_WIDTH + GROUP_STATE], axis=-1)
    xb = centred_dwconv_silu(xb, conv_w[:, GROUP_STATE:], conv_b[GROUP_STATE:])
    xs, bm = jnp.split(xb, [SSD_WIDTH], axis=-1)
    xs = xs.reshape(b, L, SSD_HEADS, SSD_HEAD_DIM)
    bm = bm.reshape(b, L, SSD_GROUPS, SSD_STATE)
    dt = softplus_dt(dt_raw, dt_bias)
    a = -jnp.exp(a_log.astype(jnp.float32))
    h_f = ssd_final_state(xs, dt[:, :, 0], a[0], bm)
    h_b = ssd_final_state(flip(xs), flip(dt[:, :, 1]), a[1], flip(bm))
    return h_f, h_b


def setup_inputs(seed: int = 0) -> dict:
    key = jax.random.key(seed)
    ks = iter(jax.random.split(key, 32))
    f32 = jnp.float32
    L = DEPTH

    def normal(shape, scale=1.0):
        return scale * jax.random.normal(next(ks), shape, f32)

    def gain(shape):
        return 1.0 + 0.1 * jax.random.normal(next(ks), shape, f32)

    d = {}
    d['x'] = normal((BATCH, SEQ, D_MODEL))
    d['c'] = normal((BATCH, D_MODEL))
    d['ctx'] = normal((BATCH, CTX_LEN, D_MODEL))
    d['c_ctx'] = normal((D_MODEL,))
    d['ada_w'] = normal((L, D_MODEL, N_MOD * D_MODEL), D_MODEL ** -0.5)
    d['ada_b'] = normal((L, N_MOD * D_MODEL), 0.02)
    d['ffn1_norm_pre'] = gain((L, D_MODEL))
    d['ffn1_norm_post'] = gain((L, D_MODEL))
    d['ffn1_w_gate'] = normal((L, D_MODEL, D_FF), D_MODEL ** -0.5)
    d['ffn1_w_up'] = normal((L, D_MODEL, D_FF), D_MODEL ** -0.5)
    d['ffn1_w_down'] = normal((L, D_FF, D_MODEL), D_FF ** -0.5)
    d['mix_norm_pre'] = gain((L, D_MODEL))
    d['mix_norm_post'] = gain((L, D_MODEL))
    d['w_in'] = normal((L, D_MODEL, PROJ_WIDTH), D_MODEL ** -0.5)
    d['fourier_w'] = normal((L, FOURIER_HEADS, FOURIER_HEAD_DIM, FOURIER_HEAD_DIM), FOURIER_HEAD_DIM ** -0.5)
    d['conv_w'] = normal((L, CONV_WIDTH, CONV_DIM), CONV_WIDTH ** -0.5)
    d['conv_b'] = normal((L, CONV_DIM), 0.02)
    dt_init = jnp.exp(jax.random.uniform(next(ks), (L, 2, SSD_HEADS), f32, math.log(1e-3), math.log(1e-1)))
    d['dt_bias'] = dt_init + jnp.log(-jnp.expm1(-dt_init))
    d['a_log'] = jnp.log(jax.random.uniform(next(ks), (L, 2, SSD_HEADS), f32, 1.0, 16.0))
    d['d_skip'] = gain((L, SSD_HEADS))
    d['ssd_norm'] = gain((L, SSD_WIDTH))
    d['w_out'] = normal((L, MIX_WIDTH, D_MODEL), MIX_WIDTH ** -0.5)
    d['ffn2_norm_pre'] = gain((L, D_MODEL))
    d['ffn2_norm_post'] = gain((L, D_MODEL))
    d['ffn2_w_gate'] = normal((L, D_MODEL, D_FF), D_MODEL ** -0.5)
    d['ffn2_w_up'] = normal((L, D_MODEL, D_FF), D_MODEL ** -0.5)
    d['ffn2_w_down'] = normal((L, D_FF, D_MODEL), D_FF ** -0.5)
    return d


def reference(x, c, ctx, c_ctx, ada_w, ada_b, ffn1_norm_pre, ffn1_norm_post, ffn1_w_gate, ffn1_w_up,
              ffn1_w_down, mix_norm_pre, mix_norm_post, w_in, fourier_w, conv_w, conv_b, dt_bias, a_log,
              d_skip, ssd_norm, w_out, ffn2_norm_pre, ffn2_norm_post, ffn2_w_gate, ffn2_w_up, ffn2_w_down):
    b = x.shape[0]
    for layer in range(DEPTH):
        mod = (jax.nn.silu(c) @ ada_w[layer] + ada_b[layer]).reshape(b, N_MOD, D_MODEL)
        mod_ctx = (jax.nn.silu(c_ctx) @ ada_w[layer] + ada_b[layer]).reshape(1, N_MOD, D_MODEL)
        ffn1 = (ffn1_norm_pre[layer], ffn1_norm_post[layer], ffn1_w_gate[layer], ffn1_w_up[layer], ffn1_w_down[layer])
        ffn2 = (ffn2_norm_pre[layer], ffn2_norm_post[layer], ffn2_w_gate[layer], ffn2_w_up[layer], ffn2_w_down[layer])
        mix = (mix_norm_pre[layer], mix_norm_post[layer], w_in[layer], fourier_w[layer], conv_w[layer],
               conv_b[layer], dt_bias[layer], a_log[layer], d_skip[layer], ssd_norm[layer], w_out[layer])

        x = macaron_ffn(x, mod[:, 0:3], *ffn1)
        ctx = macaron_ffn(ctx, mod_ctx[:, 0:3], *ffn1)

        if layer + 1 < DEPTH:
            zeros = jnp.zeros((b, SSD_HEADS, SSD_HEAD_DIM, SSD_STATE), jnp.float32)
            ctx, h_ctx_f, h_ctx_b = mix_sublayer(ctx, mod_ctx[:, 3:6], zeros, zeros, *mix)
        else:
            h_ctx_f, h_ctx_b = context_states(ctx, mod_ctx[:, 3:6], mix_norm_pre[layer], w_in[layer],
                                              conv_w[layer], conv_b[layer], dt_bias[layer], a_log[layer])

        x, _, _ = mix_sublayer(x, mod[:, 3:6], h_ctx_f, h_ctx_b, *mix)

        x = macaron_ffn(x, mod[:, 6:9], *ffn2)
        if layer + 1 < DEPTH:
            ctx = macaron_ffn(ctx, mod_ctx[:, 6:9], *ffn2)
    return x
```

```python
import os
import numpy as np
import concourse.bass as bass
import concourse.mybir as mybir
from concourse.bass_utils import run_bass_kernel_spmd

F32 = mybir.dt.float32
BF16 = mybir.dt.bfloat16
AF = mybir.ActivationFunctionType
ALU = mybir.AluOpType

PE, ACT, DVE, POOL, SP = "tensor", "scalar", "vector", "gpsimd", "sync"
ENGS = (PE, ACT, DVE, POOL, SP)

D = 1024
L = 4096
LC = 256
DFF = 2816
NF = 22
PROJ = 2840
EPS = 1e-6


class Res:
    __slots__ = ("w", "rs")

    def __init__(self):
        self.w = None
        self.rs = {}


class Chan:
    __slots__ = ("sem", "count", "last", "bg")

    def __init__(self, sem):
        self.sem = sem
        self.count = 0
        self.last = None
        self.bg = False


class Ins:
    __slots__ = ("eng", "fn", "deps", "chan", "dval", "needed", "semval", "odeps", "n", "fin", "done", "barrier")

    def __init__(self, eng, fn, chan):
        self.eng = eng
        self.fn = fn
        self.chan = chan
        self.odeps = []
        self.n = 256
        self.fin = 0.0
        self.done = False
        self.barrier = False
        self.deps = []
        self.dval = None
        self.needed = False
        self.semval = None


class _Probe:
    def __getattr__(self, name):
        def f(*a, **k):
            out = k.get("out", a[0] if a else None)
            return out
        return f


def _free_elems(fn):
    try:
        out = fn(_Probe())
        shp = out.shape
        n = 1
        for d in shp[1:]:
            n *= int(d)
        return n, out.dtype
    except Exception:
        return None, None


class Prog:
    def __init__(self):
        self.q = {e: [] for e in ENGS}
        self.chans = []
        self.last_real = {}
        self.seq = []

    def chan(self, sem):
        c = Chan(sem)
        self.chans.append(c)
        return c

    def op(self, eng, fn, r=(), w=(), chan=None, n=None, after=()):
        ins = Ins(eng, fn, chan)
        if n is None:
            n, dt = _free_elems(fn)
            if n is None:
                n = 256
            elif chan is not None:
                n = n * (2 if dt == BF16 else 4)
        ins.n = n
        deps = list(after)
        isdma = chan is not None
        for res in r:
            d = res.w
            if d is not None and (isdma or d.chan is not None or d.eng != eng or eng != PE):
                deps.append(d)
        for res in w:
            d = res.w
            if d is not None:
                if isdma or d.chan is not None or d.eng != eng or eng != PE:
                    deps.append(d)
                else:
                    ins.odeps.append(d)
            for d in res.rs.values():
                if isdma or d.chan is not None or d.eng != eng or eng != PE:
                    deps.append(d)
        if isdma and chan.last is not None:
            ins.odeps.append(chan.last)
        seen = set()
        for d in deps:
            if id(d) not in seen and d is not ins:
                seen.add(id(d))
                ins.deps.append(d)
                d.needed = True
        for res in r:
            key = id(chan) if isdma else eng
            prev = res.rs.get(key)
            if prev is not None and prev is not ins:
                ins.odeps.append(prev)
            res.rs[key] = ins
        for res in w:
            res.w = ins
            res.rs = {}
        if isdma:
            chan.count += 16
            ins.dval = chan.count
            chan.last = ins
        self.seq.append(ins)
        return ins

    def barrier(self):
        self.seq.append(None)

    def schedule(self):
        segs, cur = [], []
        for ins in self.seq:
            if ins is None:
                segs.append(cur)
                cur = []
            else:
                cur.append(ins)
        if cur:
            segs.append(cur)
        self.q = {e: [] for e in ENGS}
        chan_last = {}
        for seg in segs:
            order = self.schedule_segment(seg)
            last = {}
            for ins in order:
                self.q[ins.eng].append(ins)
                if ins.chan is None:
                    last[ins.eng] = ins
                elif not ins.chan.bg:
                    chan_last[id(ins.chan)] = ins
            lasts = list(last.values()) + list(chan_last.values())
            for e in ENGS:
                b = Ins(e, lambda en: en.nop(nofuse=True), None)
                for d in lasts:
                    if d.chan is None and d.eng == e:
                        continue
                    b.deps.append(d)
                    d.needed = True
                self.q[e].append(b)

    @staticmethod
    def cost(ins):
        n = ins.n
        if ins.chan is not None:
            return 600.0, 2500.0 + n * 0.5
        if ins.eng == PE:
            return 60.0 + n * 0.33, None
        if ins.eng == POOL:
            return 1400.0 + n * 0.3, None
        if ins.eng == ACT:
            return 180.0 + n * 0.95, None
        return 110.0 + n * 0.8, None

    def schedule_segment(self, seg):
        W = int(os.environ.get("K_W", "40"))
        pend = {e: [] for e in ENGS}
        segset = set(seg)
        for ins in seg:
            ins.done = False
            pend[ins.eng].append(ins)
        head = {e: 0 for e in ENGS}
        free = {e: 0.0 for e in ENGS}
        order = []
        total = len(seg)
        while len(order) < total:
            best = None
            for e in ENGS:
                lst = pend[e]
                i = head[e]
                while i < len(lst) and lst[i].done:
                    i += 1
                head[e] = i
                cnt = 0
                while i < len(lst) and cnt < W:
                    ins = lst[i]
                    i += 1
                    if ins.done:
                        continue
                    cnt += 1
                    ok = True
                    t = free[e]
                    for d in ins.deps:
                        if not d.done:
                            if d in segset:
                                ok = False
                                break
                        elif d.fin > t:
                            t = d.fin
                    if not ok:
                        continue
                    for d in ins.odeps:
                        if not d.done and d in segset:
                            ok = False
                            break
                    if not ok:
                        continue
                    if best is None or t < best[0] - 1e-9:
                        best = (t, ins)
                    if t <= free[e] + 1e-9:
                        break
            t, ins = best
            occ, lat = self.cost(ins)
            free[ins.eng] = t + occ
            ins.fin = t + (lat if lat is not None else occ + float(os.environ.get('K_HOP', '60')))
            ins.done = True
            order.append(ins)
        return order

    def emit(self, block, psem):
        self.schedule()
        for e in ENGS:
            c = 0
            for ins in self.q[e]:
                if ins.chan is None and ins.needed:
                    c += 1
                    ins.semval = c
        prog = self

        def make(e):
            def body(en):
                known = {}
                for ins in prog.q[e]:
                    need = {}
                    for d in ins.deps:
                        if d.chan is not None:
                            sem, val = d.chan.sem, d.dval
                        else:
                            sem, val = psem[d.eng], d.semval
                        k = id(sem)
                        if known.get(k, 0) < val and need.get(k, (None, 0))[1] < val:
                            need[k] = (sem, val)
                    for k, (sem, val) in need.items():
                        en.wait_ge(sem, val)
                        known[k] = val
                    bi = ins.fn(en)
                    if ins.chan is not None:
                        bi.then_inc(ins.chan.sem, 16)
                    elif ins.needed:
                        bi.then_inc(psem[e], 1)
            return body

        for e in ENGS:
            if self.q[e]:
                getattr(block, e)(make(e))


class Tile:
    __slots__ = ("ap", "res", "chan", "kind")

    def __init__(self, ap, chan=None):
        self.kind = None
        self.ap = ap
        self.res = Res()
        self.chan = chan


ARENA_WORDS = 51800


class Builder:
    def __init__(self, debug=()):
        self.debug = set(debug)
        nc = bass.Bass("TRN2", target_bir_lowering=False)
        self.nc = nc
        self.P = Prog()
        self.cms = []

        def enter(cm):
            self.cms.append(cm)
            return cm.__enter__()
        self.arena = enter(nc.sbuf_tensor("arena", [128, ARENA_WORDS], F32))
        self.top = 0
        nsem = 96
        self.sems = [enter(nc.semaphore(f"s{i}")) for i in range(nsem)]
        self.psem = {e: self.sems[i] for i, e in enumerate(ENGS)}
        self.free_chans = {"hw": [self.P.chan(s) for s in self.sems[5:50]],
                           "sw": [self.P.chan(s) for s in self.sems[50:]]}
        self.banks = [Tile(enter(nc.psum_tensor(f"bank{i}", [128, 512], F32))[:, :]) for i in range(8)]
        self.ins = {}
        self.outs = {}

    def din(self, name, shape):
        ap = self.nc.dram_tensor(name, list(shape), F32, kind="ExternalInput").ap()
        self.ins[name] = ap
        return ap

    def dout(self, name, shape, dt=F32):
        ap = self.nc.dram_tensor(name, list(shape), dt, kind="ExternalOutput").ap()
        self.outs[name] = ap
        return ap

    def dscratch(self, name, shape, dt=F32):
        if name in self.debug:
            return self.dout(name, shape, dt)
        return self.nc.dram_tensor(name, list(shape), dt, kind="Internal").ap()

    def tile(self, shape, dt=F32, chan=False):
        n = int(np.prod(shape[1:]))
        nb = n * (2 if dt == BF16 else 4)
        n4 = (nb + 63) // 64 * 16
        o = self.top
        self.top += n4
        assert self.top <= ARENA_WORDS, f"SBUF arena overflow {self.top}"
        ap = self.arena[:, o:o + n4]
        if dt == BF16:
            ap = ap.bitcast(BF16)
        ap = ap[:, 0:n]
        if len(shape) == 3:
            ap = ap.rearrange("p (a b) -> p a b", a=shape[1])
        elif len(shape) == 4:
            ap = ap.rearrange("p (a b c) -> p a b c", a=shape[1], b=shape[2])
        t = Tile(ap, self.free_chans[chan].pop() if chan else None)
        t.kind = chan
        return t

    def release(self, mark, tiles):
        self.top = mark
        for t in tiles:
            if t.chan is not None:
                self.free_chans[t.kind].append(t.chan)
                t.chan = None

    def dma(self, eng, dst_ap, src_ap, chan, r=(), w=(), after=(), n=None):
        return self.P.op(eng, lambda e: e.dma_start(out=dst_ap, in_=src_ap), r=r, w=w, chan=chan, after=after, n=n)

    def load(self, eng, t, dst_ap, src_ap, after=(), n=None):
        return self.dma(eng, dst_ap, src_ap, t.chan, w=[t.res], after=after, n=n)

    def convert_weights(self, name, wg_d, wu_d, wd_d):
        outs, todo, done = [], [], []
        for nm, src, rows, cols in (("g", wg_d, D, DFF), ("u", wu_d, D, DFF), ("d", wd_d, DFF, D)):
            dst = self.nc.dram_tensor(f"{name}_w{nm}_bf", [rows, cols], BF16, kind="Internal").ap()
            ch = self.free_chans["sw"].pop()
            ch.bg = True
            for r0 in range(0, rows, 128):
                todo.append(lambda after, dst=dst, src=src, r0=r0, ch=ch, cols=cols: done.append(
                    self.dma(POOL, dst[r0:r0 + 128, :], src[r0:r0 + 128, :], ch, n=cols * 2, after=after)))
            outs.append(dst)
        return outs, todo, done

    def store(self, eng, t, dst_ap, src_ap):
        return self.dma(eng, dst_ap, src_ap, t.chan, r=[t.res])

    def convert_weights_bycol(self, name, wg_d, wu_d, wd_d):
        outs, deps = [], {}
        for nm, src, rows, cols, blk in (("g", wg_d, D, DFF, 256), ("u", wu_d, D, DFF, 256), ("d", wd_d, DFF, D, 128)):
            dst = self.nc.dram_tensor(f"{name}_w{nm}_bf", [rows, cols], BF16, kind="Internal").ap()
            outs.append(dst)
            deps[nm] = []
        order = []
        for g in range(11):
            order += [("g", g, 256), ("u", g, 256)]
        order += [("d", dch, 128) for dch in range(8)]
        srcs = {"g": wg_d, "u": wu_d, "d": wd_d}
        dsts = dict(zip("gud", outs))
        for nm, i, blk in order:
            ch = self.free_chans["sw"].pop()
            ch.bg = True
            ins = self.dma(POOL, dsts[nm][:, i * blk:(i + 1) * blk], srcs[nm][:, i * blk:(i + 1) * blk], ch, n=blk * 2 * 8)
            deps[nm].append(ins)
        return outs, deps

    def setup_consts(self):
        P = self.P
        self.ident = self.tile([128, 128], BF16)
        self.ones = self.tile([128, 128], BF16)
        tmp = self.tile([128, 128], F32)
        self.identf = tmp
        P.op(POOL, lambda e: e.memset(tmp.ap, 0.0), w=[tmp.res])
        P.op(POOL, lambda e: e.affine_select(tmp.ap, tmp.ap, [[-1, 128]], ALU.not_equal, 1.0, base=0,
                                             channel_multiplier=1), r=[tmp.res], w=[tmp.res])
        P.op(POOL, lambda e: e.tensor_copy(self.ident.ap, tmp.ap), r=[tmp.res], w=[self.ident.res])
        P.op(POOL, lambda e: e.memset(self.ones.ap, 1.0), w=[self.ones.res])

    def phase_mod(self, background=None):
        P = self.P
        cc_d = self.din("cc", [128, 8, 2])
        adaw = self.din("ada_w", [D, 9 * D])
        adab_d = self.din("ada_b", [128, 72])
        norms_d = self.din("norms", [128, 6, 8])
        self.mod = self.tile([128, 72, 2])
        self.norms = self.tile([128, 6, 8], chan="hw")
        self.coef = {}
        adab = self.tile([128, 72], chan="hw")
        ctiles = [(self.tile([128, 8, 2]), self.tile([128, 8, 2])) for _ in range(3)]
        mark = self.top
        cc = self.tile([128, 8, 2], chan="hw")
        scc = self.tile([128, 8, 2])
        wt = [self.tile([128, 8, 512], F32, chan="hw") for _ in range(3)]
        modrow = self.tile([128, 9 * D], F32)
        self.load(SP, cc, cc.ap, cc_d)
        self.load(SP, adab, adab.ap, adab_d)
        self.load(SP, self.norms, self.norms.ap, norms_d)
        P.op(ACT, lambda e: e.activation(scc.ap, cc.ap, AF.Silu), r=[cc.res], w=[scc.res])
        bank = self.banks[0]
        adaw_v = adaw.rearrange("(k p) n -> p k n", p=128)
        for g in range(18):
            t = wt[g % 3]
            self.load(SP, t, t.ap, adaw_v[:, :, g * 512:(g + 1) * 512])
            bk = self.banks[1 + g % 4]
            for k in range(8):
                P.op(PE, lambda e, t=t, k=k, bk=bk: e.matmul(bk.ap[0:2, :], scc.ap[:, k, :], t.ap[:, k, :], start=(k == 0),
                                                           stop=(k == 7)), r=[t.res, scc.res], w=[bk.res])
            if g % 2:
                P.op(ACT, lambda e, g=g, bk=bk: e.activation(modrow.ap[0:2, g * 512:(g + 1) * 512], bk.ap[0:2, :], AF.Identity),
                     r=[bk.res], w=[modrow.res])
            else:
                P.op(DVE, lambda e, g=g, bk=bk: e.tensor_copy(modrow.ap[0:2, g * 512:(g + 1) * 512], bk.ap[0:2, :]),
                     r=[bk.res], w=[modrow.res])
        for n in range(72):
            P.op(PE, lambda e, n=n: e.transpose(bank.ap[:, 2 * n:2 * n + 2], modrow.ap[0:2, n * 128:(n + 1) * 128],
                                                self.identf.ap[0:2, 0:2]), r=[modrow.res, self.identf.res], w=[bank.res])
        P.op(DVE, lambda e: e.tensor_tensor(self.mod.ap, bank.ap[:, 0:144].rearrange("p (n j) -> p n j", j=2),
                                            adab.ap.unsqueeze(2).to_broadcast([128, 72, 2]), ALU.add),
             r=[bank.res, adab.res], w=[self.mod.res])
        for si, (name, wgt) in enumerate((("ffn1", 0.5), ("mix", 1.0), ("ffn2", 0.5))):
            A, Bq = ctiles[si]
            j0 = 3 * si
            shift = self.mod.ap[:, j0 * 8:(j0 + 1) * 8, :]
            scale = self.mod.ap[:, (j0 + 1) * 8:(j0 + 2) * 8, :]
            gate = self.mod.ap[:, (j0 + 2) * 8:(j0 + 3) * 8, :]
            gpre = self.norms.ap[:, 2 * si, :].unsqueeze(2).to_broadcast([128, 8, 2])
            gpost = self.norms.ap[:, 2 * si + 1, :].unsqueeze(2).to_broadcast([128, 8, 2])
            P.op(DVE, lambda e, A=A, scale=scale, gpre=gpre: e.scalar_tensor_tensor(
                A.ap, scale, 1.0, gpre, ALU.add, ALU.mult), r=[self.mod.res, self.norms.res], w=[A.res])
            P.op(DVE, lambda e, Bq=Bq, gate=gate, gpost=gpost, wgt=wgt: e.scalar_tensor_tensor(
                Bq.ap, gate, wgt, gpost, ALU.mult, ALU.mult), r=[self.mod.res, self.norms.res], w=[Bq.res])
            self.coef[name] = dict(A=A, S=shift, B=Bq)
        if background is not None:
            background()
        P.barrier()
        self.release(mark, [cc] + wt)

    def prologue(self, x, T, sq, ssbank, rstd, tmp2, h, cf, j):
        P = self.P
        PRO = int(os.environ.get("K_PRO", "9"))
        if PRO < 1:
            return
        for k in range(8):
            if os.environ.get("K_SQ") == "dve":
                P.op(DVE, lambda e, k=k: e.tensor_tensor(sq.ap[:, k, :T], x.ap[:, k, :T], x.ap[:, k, :T], ALU.mult), r=[x.res], w=[sq.res])
            else:
                P.op(ACT, lambda e, k=k: e.activation(sq.ap[:, k, :T], x.ap[:, k, :T], AF.Square), r=[x.res], w=[sq.res])
        if PRO < 2:
            return
        for k in range(8):
            P.op(PE, lambda e, k=k: e.matmul(ssbank.ap[:, :T], self.ones.ap, sq.ap[:, k, :T], start=(k == 0),
                                             stop=(k == 7)), r=[sq.res, self.ones.res], w=[ssbank.res])
        if PRO < 3:
            return
        P.op(ACT, lambda e: e.activation(rstd.ap[:, :T], ssbank.ap[:, :T], AF.Ln, bias=EPS, scale=1.0 / D),
             r=[ssbank.res], w=[rstd.res])
        P.op(ACT, lambda e: e.activation(rstd.ap[:, :T], rstd.ap[:, :T], AF.Exp, scale=-0.5), r=[rstd.res], w=[rstd.res])
        if PRO < 4:
            return
        A, S = cf["A"], cf["S"]
        for k in range(8):
            t2 = tmp2[k % 2]
            if PRO != 6:
                P.op(DVE, lambda e, k=k, t2=t2: e.scalar_tensor_tensor(
                    t2.ap[:, :T], x.ap[:, k, :T], A.ap[:, k, j:j + 1], rstd.ap[:, :T], ALU.mult, ALU.mult),
                    r=[x.res, A.res, rstd.res], w=[t2.res])
            if PRO == 5:
                continue
            P.op(ACT, lambda e, k=k, t2=t2: e.activation(h.ap[:, k, :T], t2.ap[:, :T], AF.Identity,
                                                         bias=S[:, k, j:j + 1], scale=1.0),
                 r=[t2.res, self.mod.res], w=[h.res])

    def epilogue(self, y, xres, T, sqy, ssbank, rstd, cf, j):
        P = self.P
        for k in range(8):
            P.op(PE, lambda e, k=k: e.matmul(ssbank.ap[:, :T], self.ones.ap, sqy.ap[:, k, :T], start=(k == 0),
                                             stop=(k == 7)), r=[sqy.res, self.ones.res], w=[ssbank.res])
        P.op(ACT, lambda e: e.activation(rstd.ap[:, :T], ssbank.ap[:, :T], AF.Ln, bias=EPS, scale=1.0 / D),
             r=[ssbank.res], w=[rstd.res])
        P.op(ACT, lambda e: e.activation(rstd.ap[:, :T], rstd.ap[:, :T], AF.Exp, scale=-0.5), r=[rstd.res], w=[rstd.res])
        Bq = cf["B"]
        for k in range(8):
            P.op(DVE, lambda e, k=k: e.scalar_tensor_tensor(
                y.ap[:, k, :T], y.ap[:, k, :T], Bq.ap[:, k, j:j + 1], rstd.ap[:, :T], ALU.mult, ALU.mult),
                r=[y.res, Bq.res, rstd.res], w=[y.res])
            P.op(DVE, lambda e, k=k: e.tensor_tensor(
                y.ap[:, k, :T], y.ap[:, k, :T], xres.ap[:, k, :T], ALU.add), r=[y.res, xres.res], w=[y.res])

    def phase_ffn(self, name, jobs, wg_d, wu_d, wd_d, wdeps=(), cast=False, background=()):
        P = self.P
        cf = self.coef[name]
        mark = self.top
        xt = [self.tile([128, 8, 512], F32, chan="hw") for _ in range(2)]
        sqx = self.tile([128, 8, 512], BF16)
        sqy = self.tile([128, 8, 512], BF16)
        h = self.tile([128, 8, 512], BF16)
        a = self.tile([128, NF, 512], BF16)
        y = self.tile([128, 8, 512], F32, chan="hw")
        rstd = [self.tile([128, 512], F32) for _ in range(2)]
        tmp2 = [self.tile([128, 512], F32) for _ in range(2)]
        sg = [self.tile([128, 512], F32) for _ in range(2)]
        NGU, NWD = 4, 4
        wq, wk = (POOL, "sw") if cast else (SP, "hw")
        wgs = [self.tile([128, 8, 256], BF16, chan=wk) for _ in range(NGU)]
        wus = [self.tile([128, 8, 256], BF16, chan=wk) for _ in range(NGU)]
        wds = [self.tile([128, NF, 128], BF16, chan=wk) for _ in range(NWD)]
        background = list(background)
        per_tile = (len(background) + len(jobs) - 2) // max(1, len(jobs) - 1)
        alltiles = xt + [y] + wgs + wus + wds
        wg_v = wg_d.rearrange("(k p) f -> p k f", p=128)
        wu_v = wu_d.rearrange("(k p) f -> p k f", p=128)
        wd_v = wd_d.rearrange("(f p) d -> p f d", p=128)
        B = self.banks
        ssb_pro, ssb_epi = B[6], B[7]
        cnt = {"gu": 0, "wd": 0, "pair": 0, "yb": 0}

        jobs = [tuple(jb) + (None,) * (5 - len(jb)) for jb in jobs]
        h2 = self.tile([128, 8, 512], BF16, chan="hw")
        alltiles.append(h2)

        def load_x(i):
            src, dst, T, j, _ = jobs[i]
            t = xt[i % 2]
            self.load(SP, t, t.ap[:, :, :T], src.rearrange("(k p) t -> p k t", p=128))

        def pro(i):
            src, dst, T, j, _ = jobs[i]
            self.prologue(xt[0 if os.environ.get("K_X0") else i % 2], T, sqx, ssb_pro, rstd[0], tmp2, h, cf, j)

        CUT = int(os.environ.get("K_CUT", "9"))

        def do_tile(i, src, dst, T, j, hmdst):
            x = xt[i % 2]
            if i + 1 < len(jobs):
                load_x(i + 1)
            if CUT < 1:
                return
            for g in range(11):
                s = cnt["gu"] % NGU
                cnt["gu"] += 1
                wgt, wut = wgs[s], wus[s]
                dg = [wdeps["g"][g]] if isinstance(wdeps, dict) else wdeps
                du = [wdeps["u"][g]] if isinstance(wdeps, dict) else wdeps
                lw = self.load(wq, wgt, wgt.ap, wg_v[:, :, g * 256:(g + 1) * 256], after=dg)
                self.load(wq, wut, wut.ap, wu_v[:, :, g * 256:(g + 1) * 256], after=du)
                if g == 10:
                    for _ in range(per_tile):
                        if background:
                            background.pop(0)([lw])
                for c2 in range(2):
                    f = g * 2 + c2
                    pr = cnt["pair"] % 2
                    cnt["pair"] += 1
                    bg, bu = B[2 * pr], B[2 * pr + 1]
                    for k in range(8):
                        P.op(PE, lambda e, k=k, c2=c2, wgt=wgt, bg=bg: e.matmul(
                            bg.ap[:, :T], wgt.ap[:, k, c2 * 128:(c2 + 1) * 128], h.ap[:, k, :T], start=(k == 0),
                            stop=(k == 7)), r=[wgt.res, h.res], w=[bg.res])
                    for k in range(8):
                        P.op(PE, lambda e, k=k, c2=c2, wut=wut, bu=bu: e.matmul(
                            bu.ap[:, :T], wut.ap[:, k, c2 * 128:(c2 + 1) * 128], h.ap[:, k, :T], start=(k == 0),
                            stop=(k == 7)), r=[wut.res, h.res], w=[bu.res])
                    sgt = sg[pr]
                    P.op(ACT, lambda e, bg=bg, sgt=sgt: e.activation(sgt.ap[:, :T], bg.ap[:, :T], AF.Silu),
                         r=[bg.res], w=[sgt.res])
                    P.op(DVE, lambda e, bu=bu, sgt=sgt, f=f: e.tensor_tensor(a.ap[:, f, :T], sgt.ap[:, :T], bu.ap[:, :T],
                                                                            ALU.mult), r=[sgt.res, bu.res], w=[a.res])
            if CUT < 2:
                return
            if i + 1 < len(jobs) and not os.environ.get("K_NOSWP"):
                pro(i + 1)
            if CUT < 3:
                return
            for dch in range(8):
                s = cnt["wd"] % NWD
                cnt["wd"] += 1
                wdt = wds[s]
                dd_ = [wdeps["d"][dch]] if isinstance(wdeps, dict) else wdeps
                self.load(wq, wdt, wdt.ap, wd_v[:, :, dch * 128:(dch + 1) * 128], after=dd_)
                yb = B[4 + cnt["yb"] % 2]
                cnt["yb"] += 1
                DN = os.environ.get("K_DN", "mcs")
                if "m" in DN:
                    for f in range(NF):
                        P.op(PE, lambda e, f=f, wdt=wdt, yb=yb: e.matmul(
                            yb.ap[:, :T], wdt.ap[:, f, :], a.ap[:, f, :T], start=(f == 0), stop=(f == NF - 1)),
                            r=[wdt.res, a.res], w=[yb.res])
                if "c" in DN:
                    P.op(DVE, lambda e, dch=dch, yb=yb: e.tensor_copy(y.ap[:, dch, :T], yb.ap[:, :T]), r=[yb.res], w=[y.res])
                if "s" in DN:
                    P.op(ACT, lambda e, dch=dch: e.activation(sqy.ap[:, dch, :T], y.ap[:, dch, :T], AF.Square),
                         r=[y.res], w=[sqy.res])
            if CUT < 4:
                return
            self.epilogue(y, x, T, sqy, ssb_epi, rstd[1], cf, j)
            if CUT < 5:
                return
            self.store(SP, y, dst.rearrange("(k p) t -> p k t", p=128), y.ap[:, :, :T])
            if hmdst is not None:
                self.prologue(y, T, sqy, ssb_epi, rstd[1], tmp2, h2, self.coef["mix"], j)
                self.store(SP, h2, hmdst.rearrange("(k p) t -> p k t", p=128), h2.ap[:, :, :T])

        load_x(0)
        pro(0)
        for i, (src, dst, T, j, hmdst) in enumerate(jobs):
            do_tile(i, src, dst, T, j, hmdst)
        for bgf in background:
            bgf([])
        P.barrier()
        self.release(mark, alltiles)

    def view(self, t, ap):
        v = Tile(ap)
        v.res = t.res
        v.chan = t.chan
        v.kind = t.kind
        return v

    def setup_mix_consts(self):
        P = self.P
        nc = self.nc
        self.convw = self.tile([128, 14, 3], chan="hw")
        self.convb = self.tile([128, 14], chan="hw")
        self.dtb = self.tile([128, 24], chan="hw")
        self.avec = self.tile([128, 24], chan="hw")
        self.dsk = self.tile([128, 12], chan="hw")
        self.ssdg = self.tile([128, 768], chan="hw")
        for t, name, shp in ((self.convw, "conv_w", [128, 14, 3]), (self.convb, "conv_b", [128, 14]),
                             (self.dtb, "dt_bias", [128, 24]), (self.avec, "a_log", [128, 24]),
                             (self.dsk, "d_skip", [128, 12]), (self.ssdg, "ssd_norm", [128, 768])):
            self.load(SP, t, t.ap, self.din(name, shp))
        P.op(ACT, lambda e: e.activation(self.avec.ap, self.avec.ap, AF.Exp), r=[self.avec.res], w=[self.avec.res])
        P.op(DVE, lambda e: e.tensor_scalar(self.avec.ap, self.avec.ap, -1.0, None, ALU.mult), r=[self.avec.res],
             w=[self.avec.res])
        self.trif = self.tile([128, 128], BF16)
        self.trib = self.tile([128, 128], BF16)
        tmp = self.tile([128, 128], F32)
        for tri, pat, cm in ((self.trif, [[1, 128]], -1), (self.trib, [[-1, 128]], 1)):
            P.op(POOL, lambda e: e.memset(tmp.ap, 1.0), w=[tmp.res])
            P.op(POOL, lambda e, pat=pat, cm=cm: e.affine_select(tmp.ap, tmp.ap, pat, ALU.is_ge, 0.0, base=0,
                                                                 channel_multiplier=cm), r=[tmp.res], w=[tmp.res])
            P.op(POOL, lambda e, tri=tri: e.tensor_copy(tri.ap, tmp.ap), r=[tmp.res], w=[tri.res])
        self.negm = [self.tile([128, 4, 128], BF16) for _ in range(2)]
        tmp4 = self.tile([128, 4, 128], F32)
        for nm, pat, cm in ((self.negm[0], [[0, 4], [-1, 128]], 1), (self.negm[1], [[0, 4], [1, 128]], -1)):
            P.op(POOL, lambda e: e.memset(tmp4.ap, -30000.0), w=[tmp4.res])
            P.op(POOL, lambda e, pat=pat, cm=cm: e.affine_select(tmp4.ap, tmp4.ap, pat, ALU.is_ge, 0.0, base=-1,
                                                                 channel_multiplier=cm), r=[tmp4.res], w=[tmp4.res])
            P.op(POOL, lambda e, nm=nm: e.tensor_copy(nm.ap, tmp4.ap), r=[tmp4.res], w=[nm.res])
        self.HT = [self.tile([128, 768], F32) for _ in range(2)]
        for h in self.HT:
            P.op(POOL, lambda e, h=h: e.memset(h.ap, 0.0), w=[h.res])

    def phase_mix_in(self, srcT, Lq, j, w_in, outs, hm_src=None):
        P = self.P
        B = self.banks
        cf = self.coef["mix"]
        full = "Z" in outs
        T = min(512, Lq)
        nt = Lq // T
        nch = Lq // 128
        mark0 = self.top
        hm = self.tile([128, 8, Lq], BF16)
        if hm_src is not None:
            hm.chan = self.free_chans["hw"].pop()
            hm.kind = "hw"
            self.load(SP, hm, hm.ap, hm_src.rearrange("(k p) t -> p k t", p=128))
        else:
            mark = self.top
            xt = [self.tile([128, 8, 512], F32, chan="hw") for _ in range(2)]
            sqx = self.tile([128, 8, 512], BF16)
            rstd = self.tile([128, 512], F32)
            tmp2 = [self.tile([128, 512], F32) for _ in range(2)]
            for t in range(nt):
                x = xt[t % 2]
                self.load(SP, x, x.ap[:, :, :T], srcT[:, t * T:(t + 1) * T].rearrange("(k p) t -> p k t", p=128))
                self.prologue(x, T, sqx, B[6], rstd, tmp2, self.view(hm, hm.ap[:, :, t * T:(t + 1) * T]), cf, j)
            P.barrier()
            self.release(mark, xt)
        w_v = w_in.rearrange("(k p) n -> p k n", p=128)
        cl = list(range(14)) if full else list(range(4, 14))
        wsl = [self.tile([128, 8, 128], BF16, chan="sw") for _ in range(3)]
        urow = [self.tile([128, Lq + 2], F32) for _ in range(2)]
        acc = self.tile([128, Lq], F32)
        post = [self.tile([128, Lq], BF16, chan="hw") for _ in range(2)]
        tm = [self.tile([128, nch, 128], BF16, chan="hw") for _ in range(2)]
        for u in urow:
            P.op(POOL, lambda e, u=u: e.memset(u.ap, 0.0), w=[u.res])
        XSv = outs["XS"].rearrange("(c p) n -> p c n", p=128)
        BSv = outs["BS"].rearrange("(c p) n -> p c n", p=128)
        ntr = [0]

        def conv_chunk(idx, cc):
            ws = wsl[idx % 3]
            ur = urow[idx % 2]
            po = post[idx % 2]
            self.load(POOL, ws, ws.ap, w_v[:, :, 1024 + cc * 128:1024 + (cc + 1) * 128])
            for tt in range(nt):
                bk = B[tt % 2]
                for k in range(8):
                    P.op(PE, lambda e, k=k, tt=tt, bk=bk: e.matmul(bk.ap[:, :T], ws.ap[:, k, :], hm.ap[:, k, tt * T:(tt + 1) * T],
                                                               start=(k == 0), stop=(k == 7)), r=[ws.res, hm.res], w=[bk.res])
                if tt % 2:
                    P.op(ACT, lambda e, tt=tt, bk=bk: e.activation(ur.ap[:, 1 + tt * T:1 + (tt + 1) * T], bk.ap[:, :T],
                                                                   AF.Identity), r=[bk.res], w=[ur.res])
                else:
                    P.op(DVE, lambda e, tt=tt, bk=bk: e.tensor_copy(ur.ap[:, 1 + tt * T:1 + (tt + 1) * T], bk.ap[:, :T]),
                         r=[bk.res], w=[ur.res])
            cw = self.convw
            P.op(DVE, lambda e: e.tensor_scalar(acc.ap, ur.ap[:, 0:Lq], cw.ap[:, cc, 0:1], None, ALU.mult),
                 r=[ur.res, cw.res], w=[acc.res])
            P.op(DVE, lambda e: e.scalar_tensor_tensor(acc.ap, ur.ap[:, 1:Lq + 1], cw.ap[:, cc, 1:2], acc.ap, ALU.mult,
                                                        ALU.add), r=[ur.res, cw.res, acc.res], w=[acc.res])
            P.op(DVE, lambda e: e.scalar_tensor_tensor(acc.ap, ur.ap[:, 2:Lq + 2], cw.ap[:, cc, 2:3], acc.ap, ALU.mult,
                                                       ALU.add), r=[ur.res, cw.res, acc.res], w=[acc.res])
            P.op(ACT, lambda e: e.activation(po.ap, acc.ap, AF.Silu, bias=self.convb.ap[:, cc:cc + 1], scale=1.0),
                 r=[acc.res, self.convb.res], w=[po.res])
            if full and (cc < 4 or cc >= 10):
                row0 = cc * 128 if cc < 4 else 512 + (cc - 10) * 128
                self.store(SP, po, outs["CBT"][row0:row0 + 128, :], po.ap)
            if cc >= 4:
                tmt = tm[ntr[0] % 2]
                ntr[0] += 1
                for c0 in range(0, nch, 8):
                    n8 = min(8, nch - c0)
                    bk = B[2 + (c0 // 8) % 2]
                    bkb = bk.ap.bitcast(BF16)
                    for c in range(n8):
                        P.op(PE, lambda e, c=c, c0=c0, bkb=bkb: e.transpose(
                            bkb[:, c * 128:(c + 1) * 128], po.ap[:, (c0 + c) * 128:(c0 + c + 1) * 128], self.ident.ap),
                            r=[po.res, self.ident.res], w=[bk.res])
                    if (c0 // 8) % 2:
                        P.op(ACT, lambda e, c0=c0, n8=n8, bkb=bkb: e.activation(
                            tmt.ap[:, c0:c0 + n8, :], bkb[:, 0:n8 * 128].rearrange("p (c n) -> p c n", n=128), AF.Identity),
                            r=[bk.res], w=[tmt.res])
                    else:
                        P.op(DVE, lambda e, c0=c0, n8=n8, bkb=bkb: e.tensor_copy(
                            tmt.ap[:, c0:c0 + n8, :], bkb[:, 0:n8 * 128].rearrange("p (c n) -> p c n", n=128)),
                            r=[bk.res], w=[tmt.res])
                if cc < 10:
                    self.store(SP, tmt, XSv[:, :, (cc - 4) * 128:(cc - 3) * 128], tmt.ap)
                else:
                    self.store(SP, tmt, BSv[:, :, (cc - 10) * 128:(cc - 9) * 128], tmt.ap)

        for idx, cc in enumerate(cl):
            conv_chunk(idx, cc)
        wtok = self.tile([128, 8, 1048], BF16, chan="sw")
        if full:
            self.load(POOL, wtok, wtok.ap[:, :, 0:512], w_v[:, :, 0:512])
            self.load(POOL, wtok, wtok.ap[:, :, 512:1024], w_v[:, :, 512:1024])
        self.load(POOL, wtok, wtok.ap[:, :, 1024:1048], w_v[:, :, 2816:2840])
        ufst = [self.tile([128, 256], BF16, chan="hw") for _ in range(2)]
        zst = [self.tile([128, 768], BF16, chan="hw") for _ in range(2)]
        dtst = [self.tile([128, 24], F32, chan="hw") for _ in range(2)]
        spx2 = [[self.tile([128, 24], F32) for _ in range(3)] for _ in range(2)]

        def tok_chunk(c):
            uf, zs, ds = ufst[c % 2], zst[c % 2], dtst[c % 2]
            sl = slice(c * 128, (c + 1) * 128)
            bA, bB, bC = (B[4], B[5], B[7]) if c % 2 == 0 else (B[0], B[1], B[6])
            if full:
                for bk, c0 in ((bA, 0), (bB, 512)):
                    for k in range(8):
                        P.op(PE, lambda e, k=k, bk=bk, c0=c0: e.matmul(bk.ap, hm.ap[:, k, sl], wtok.ap[:, k, c0:c0 + 512],
                                                                     start=(k == 0), stop=(k == 7)),
                             r=[hm.res, wtok.res], w=[bk.res])
                P.op(ACT, lambda e: e.activation(uf.ap, bA.ap[:, 0:256], AF.Identity), r=[bA.res], w=[uf.res])
                P.op(ACT, lambda e: e.activation(zs.ap[:, 0:256], bA.ap[:, 256:512], AF.Silu), r=[bA.res], w=[zs.res])
                P.op(ACT, lambda e: e.activation(zs.ap[:, 256:768], bB.ap, AF.Silu), r=[bB.res], w=[zs.res])
                self.store(SP, uf, outs["UF"][sl, :], uf.ap)
                self.store(SP, zs, outs["Z"][sl, :], zs.ap)
            bk = bC
            for k in range(8):
                P.op(PE, lambda e, k=k: e.matmul(bk.ap[:, 0:24], hm.ap[:, k, sl], wtok.ap[:, k, 1024:1048], start=(k == 0),
                                                 stop=(k == 7)), r=[hm.res, wtok.res], w=[bk.res])
            P.op(DVE, lambda e: e.tensor_tensor(xball.ap[:, c, :], bk.ap[:, 0:24], self.dtb.ap, ALU.add),
                 r=[bk.res, self.dtb.res], w=[xball.res])

        xball = self.tile([128, nch, 24], F32)
        axall = self.tile([128, nch, 24], F32)
        dsall = self.tile([128, nch, 24], F32, chan="hw")
        for c in range(nch):
            tok_chunk(c)
        P.op(ACT, lambda e: e.activation(axall.ap, xball.ap, AF.Abs), r=[xball.res], w=[axall.res])
        P.op(ACT, lambda e: e.activation(axall.ap, axall.ap, AF.Exp, scale=-1.0), r=[axall.res], w=[axall.res])
        P.op(ACT, lambda e: e.activation(axall.ap, axall.ap, AF.Ln, bias=1.0, scale=1.0), r=[axall.res], w=[axall.res])
        P.op(DVE, lambda e: e.scalar_tensor_tensor(dsall.ap, xball.ap, 0.0, axall.ap, ALU.max, ALU.add),
             r=[xball.res, axall.res], w=[dsall.res])
        self.store(SP, dsall, outs["DT"].rearrange("(c p) n -> p c n", p=128), dsall.ap)
        P.barrier()
        self.release(mark0, post + tm + wsl + [wtok] + ufst + zst + dtst + [hm, dsall])

    def phase_states(self, defer=False):
        P, B, sc = self.P, self.banks, self.sc
        mark = self.top
        hsst = [self.tile([128, 768], BF16, chan="hw") for _ in range(2)]
        xs_s = [self.tile([128, 768], BF16, chan="hw") for _ in range(2)]
        bs_s = [self.tile([128, 512], BF16, chan="hw") for _ in range(2)]
        dt_s = [self.tile([128, 24], F32, chan="hw") for _ in range(2)]
        NB = 3
        dtab_s = [self.tile([128, 12], BF16) for _ in range(NB)]
        cst_s = [self.tile([128, 24], F32) for _ in range(NB)]
        dd_s = [self.tile([128, 12], F32) for _ in range(NB)]
        Wt_s = [self.tile([128, 12], F32) for _ in range(NB)]
        Td_s = [self.tile([128, 12], F32) for _ in range(NB)]
        coef_s = [self.tile([128, 12], F32) for _ in range(NB)]
        xw_s = [self.tile([128, 768], BF16) for _ in range(NB)]
        cnt = [0]

        def step(d, sq, c):
            i = cnt[0]
            cnt[0] += 1
            HT = self.HT[d]
            tri = self.trif if d == 0 else self.trib
            xs, bs, dt = xs_s[i % 2], bs_s[i % 2], dt_s[i % 2]
            dtab, cst, dd, Wt, Td, coef, xw = (t[i % NB] for t in (dtab_s, cst_s, dd_s, Wt_s, Td_s, coef_s, xw_s))
            Bs = B[(i % 2) * 3:(i % 2) * 3 + 3]
            sl = slice(c * 128, (c + 1) * 128)
            sfx = "c" if sq == "c" else "x"
            self.load(SP, xs, xs.ap, sc["XS" + sfx][sl, :])
            self.load(SP, bs, bs.ap, sc["BS" + sfx][sl, :])
            self.load(SP, dt, dt.ap, sc["DT" + sfx][sl, :])
            dsl = slice(d * 12, (d + 1) * 12)
            P.op(DVE, lambda e: e.tensor_tensor(dtab.ap, dt.ap[:, dsl], self.avec.ap[:, dsl], ALU.mult),
                 r=[dt.res, self.avec.res], w=[dtab.res])
            P.op(PE, lambda e: e.matmul(Bs[0].ap[:, 0:12], tri.ap, dtab.ap, start=True, stop=True), r=[tri.res, dtab.res],
                 w=[Bs[0].res])
            P.op(PE, lambda e: e.matmul(Bs[0].ap[:, 12:24], self.ones.ap, dtab.ap, start=True, stop=True),
                 r=[self.ones.res, dtab.res], w=[Bs[0].res])
            P.op(ACT, lambda e: e.activation(cst.ap, Bs[0].ap[:, 0:24], AF.Identity), r=[Bs[0].res], w=[cst.res])
            P.op(DVE, lambda e: e.tensor_tensor(dd.ap, cst.ap[:, 12:24], cst.ap[:, 0:12], ALU.subtract), r=[cst.res],
                 w=[dd.res])
            P.op(ACT, lambda e: e.activation(Wt.ap, dd.ap, AF.Exp), r=[dd.res], w=[Wt.res])
            P.op(ACT, lambda e: e.activation(Td.ap, cst.ap[:, 12:24], AF.Exp), r=[cst.res], w=[Td.res])
            P.op(DVE, lambda e: e.tensor_tensor(coef.ap, dt.ap[:, dsl], Wt.ap, ALU.mult), r=[dt.res, Wt.res], w=[coef.res])
            P.op(POOL, lambda e: e.tensor_tensor(xw.ap.rearrange("p (h q) -> p h q", h=12),
                                                 xs.ap.rearrange("p (h q) -> p h q", h=12),
                                                 coef.ap.unsqueeze(2).to_broadcast([128, 12, 64]), ALU.mult),
                 r=[xs.res, coef.res], w=[xw.res])
            if sq == "x":
                st = hsst[c % 2]
                P.op(ACT, lambda e: e.activation(st.ap, HT.ap, AF.Identity), r=[HT.res], w=[st.res])
                self.store(SP, st, sc["HS"][d, c], st.ap)
            for g in range(4):
                bk = Bs[1 + g // 2]
                P.op(PE, lambda e, g=g, bk=bk: e.matmul(bk.ap[:, (g % 2) * 192:(g % 2 + 1) * 192], bs.ap[:, g * 128:(g + 1) * 128],
                                                       xw.ap[:, g * 192:(g + 1) * 192], start=True, stop=True),
                     r=[bs.res, xw.res], w=[bk.res])
            P.op(DVE, lambda e: e.tensor_tensor(HT.ap.rearrange("p (h q) -> p h q", h=12),
                                                HT.ap.rearrange("p (h q) -> p h q", h=12),
                                                Td.ap.unsqueeze(2).to_broadcast([128, 12, 64]), ALU.mult),
                 r=[HT.res, Td.res], w=[HT.res])
            for hh in range(2):
                P.op(DVE, lambda e, hh=hh: e.tensor_tensor(HT.ap[:, hh * 384:(hh + 1) * 384], HT.ap[:, hh * 384:(hh + 1) * 384],
                                                         Bs[1 + hh].ap[:, 0:384], ALU.add), r=[HT.res, Bs[1 + hh].res],
                     w=[HT.res])

        for d in range(2):
            order = [("c", 0), ("c", 1)] + [("x", i) for i in range(32)]
            if d == 1:
                order = [("c", 1), ("c", 0)] + [("x", 31 - i) for i in range(32)]
            for sq, c in order:
                step(d, sq, c)
        if defer:
            return mark, hsst + xs_s + bs_s + dt_s
        P.barrier()
        self.release(mark, hsst + xs_s + bs_s + dt_s)

    def phase_fourier(self, fT):
        P, B, sc = self.P, self.banks, self.sc
        T1_d = self.din("T1", [64, 64 * 128])
        V3_d = self.din("V3", [128, 128])
        CS_d = self.din("CSbd", [2, 128, 128])
        W_d = self.din("fourier_wbd", [2, 128, 128])
        mark = self.top
        T1 = self.tile([128, 64, 128], BF16, chan="sw")
        V3 = self.tile([128, 128], BF16, chan="sw")
        CS = self.tile([128, 2, 128], BF16, chan="sw")
        Wbd = self.tile([128, 2, 128], BF16, chan="sw")
        BD = self.tile([128, 2, 2, 128], BF16)
        X1 = self.tile([128, 64, 256], BF16, chan="hw")
        A1 = self.tile([128, 64, 256], BF16, chan="hw")
        X3 = self.tile([128, 64, 256], BF16, chan="hw")
        PT = self.tile([128, 2, 2, 4096], BF16)
        self.load(POOL, T1, T1.ap[0:64], T1_d.rearrange("a (l k) -> a l k", k=128))
        self.load(POOL, V3, V3.ap, V3_d)
        self.load(POOL, CS, CS.ap, CS_d.rearrange("c p n -> p c n"))
        self.load(POOL, Wbd, Wbd.ap, W_d.rearrange("c p n -> p c n"))
        self.load(SP, X1, X1.ap[0:64], sc["UF"].rearrange("(a l) e -> a l e", l=64))
        for eb in range(2):
            for cs in range(2):
                bk = B[eb * 2 + cs]
                P.op(PE, lambda e, eb=eb, cs=cs, bk=bk: e.matmul(bk.ap[:, 0:128], CS.ap[:, cs, :], Wbd.ap[:, eb, :],
                                                               start=True, stop=True), r=[CS.res, Wbd.res], w=[bk.res])
                P.op(ACT, lambda e, eb=eb, cs=cs, bk=bk: e.activation(BD.ap[:, eb, cs, :], bk.ap[:, 0:128], AF.Identity,
                                                                      scale=1.0 / 512.0), r=[bk.res], w=[BD.res])
        for l2 in range(64):
            bk = B[4 + (l2 // 2) % 2]
            P.op(PE, lambda e, l2=l2, bk=bk: e.matmul(bk.ap[:, (l2 % 2) * 256:(l2 % 2 + 1) * 256], T1.ap[0:64, l2, :],
                                                    X1.ap[0:64, l2, :], start=True, stop=True), r=[T1.res, X1.res],
                 w=[bk.res])
            if l2 % 2 == 1:
                src = bk.ap.rearrange("p (a e) -> p a e", a=2)
                if (l2 // 2) % 2:
                    P.op(ACT, lambda e, l2=l2, src=src: e.activation(A1.ap[:, l2 - 1:l2 + 1, :], src, AF.Identity),
                         r=[bk.res], w=[A1.res])
                else:
                    P.op(DVE, lambda e, l2=l2, src=src: e.tensor_copy(A1.ap[:, l2 - 1:l2 + 1, :], src), r=[bk.res],
                         w=[A1.res])
        AS = sc["AS"]
        asr = Res()
        self.dma(SP, AS.rearrange("r k l e -> (r k) l e"), A1.ap, A1.chan, r=[A1.res], w=[asr])
        for r_ in range(2):
            self.P.op(SP, lambda e, r_=r_: e.dma_start(out=X3.ap[r_ * 64:(r_ + 1) * 64], in_=AS[r_].rearrange("k l e -> l k e")),
                      r=[asr], w=[X3.res], chan=X3.chan)
        grp = 0
        for eb in range(2):
            for k0 in range(0, 64, 4):
                bk = B[6 + grp % 2]
                for a in range(4):
                    P.op(PE, lambda e, eb=eb, k1=k0 + a, a=a, bk=bk: e.matmul(
                        bk.ap[:, a * 128:(a + 1) * 128], X3.ap[:, k1, eb * 128:(eb + 1) * 128], V3.ap, start=True, stop=True),
                        r=[X3.res, V3.res], w=[bk.res])
                for r_ in range(2):
                    src = bk.ap.rearrange("p (a r k) -> p r k a", a=4, r=2)[:, r_]
                    dst = PT.ap[:, eb, r_].rearrange("p (k a) -> p k a", a=64)[:, :, k0:k0 + 4]
                    if grp % 2:
                        P.op(ACT, lambda e, src=src, dst=dst: e.activation(dst, src, AF.Identity), r=[bk.res], w=[PT.res])
                    else:
                        P.op(DVE, lambda e, src=src, dst=dst: e.tensor_copy(dst, src), r=[bk.res], w=[PT.res])
                grp += 1
        for eb in range(2):
            for kt in range(8):
                bk = B[(eb * 8 + kt) % 4]
                for cs in range(2):
                    P.op(PE, lambda e, eb=eb, kt=kt, cs=cs, bk=bk: e.matmul(
                        bk.ap, BD.ap[:, eb, cs, :], PT.ap[:, eb, cs, kt * 512:(kt + 1) * 512], start=(cs == 0), stop=(cs == 1)),
                        r=[BD.res, PT.res], w=[bk.res])
                if kt % 2:
                    P.op(ACT, lambda e, eb=eb, kt=kt, bk=bk: e.activation(fT.ap[:, eb, kt * 512:(kt + 1) * 512], bk.ap,
                                                                        AF.Identity), r=[bk.res], w=[fT.res])
                else:
                    P.op(DVE, lambda e, eb=eb, kt=kt, bk=bk: e.tensor_copy(fT.ap[:, eb, kt * 512:(kt + 1) * 512], bk.ap),
                         r=[bk.res], w=[fT.res])
        P.barrier()
        self.release(mark, [T1, V3, CS, Wbd, X1, A1, X3])

    def phase_ssd_out(self, x1T, x2T, w_out_d):
        P, B, sc = self.P, self.banks, self.sc
        cf = self.coef["mix"]
        fT = self.fT
        mark = self.top
        wout = self.tile([128, 8, 1024], BF16, chan="sw")
        wo_v = w_out_d.rearrange("(k p) n -> p k n", p=128)
        for hh in range(2):
            self.load(POOL, wout, wout.ap[:, :, hh * 512:(hh + 1) * 512], wo_v[:, :, hh * 512:(hh + 1) * 512])
        mk = lambda shp, dt=F32, chan=None: [self.tile(shp, dt, chan=chan) for _ in range(2)]
        ct_s, bt_s = mk([128, 4, 128], BF16, "hw"), mk([128, 4, 128], BF16, "hw")
        xs_s, zz_s = mk([128, 768], BF16, "hw"), mk([128, 768], BF16, "hw")
        dt_s = mk([128, 24], F32, "hw")
        hf_s, hb_s = mk([128, 768], BF16, "hw"), mk([128, 768], BF16, "hw")
        dtA_s, dtAb_s, negcs_s = mk([128, 24]), mk([128, 24], BF16), mk([128, 24])
        Ecs_s = mk([128, 24])
        cb_s = mk([128, 4, 128], BF16)
        R_s = mk([128, 24, 128], BF16)
        seg_s = mk([128, 24, 128], BF16)
        xdt_s = mk([128, 2, 768], BF16)
        yacc_s, t1_s, sq_s = mk([128, 768]), mk([128, 768]), mk([128, 768])
        ss_s, rs_s = mk([128, 4]), mk([128, 4])
        yn_s = mk([128, 768], BF16)
        YT = mk([128, 6, 512], BF16)
        o = self.tile([128, 8, 512], F32, chan="hw")
        sqo = self.tile([128, 8, 512], BF16)
        xres = self.tile([128, 8, 512], F32, chan="hw")
        rstd = self.tile([128, 512])
        CBT = sc["CBT"]
        h12 = lambda ap: ap.rearrange("p (h q) -> p h q", h=12)

        def outproj(tt):
            yt = YT[tt % 2]
            tsl = slice(tt * 512, (tt + 1) * 512)
            self.load(SP, xres, xres.ap, x1T[:, tsl].rearrange("(k p) t -> p k t", p=128))
            for dch in range(8):
                bk = B[6 + dch % 2]
                for k in range(8):
                    rhs = fT.ap[:, k, tsl] if k < 2 else yt.ap[:, k - 2, :]
                    P.op(PE, lambda e, k=k, dch=dch, bk=bk, rhs=rhs: e.matmul(
                        bk.ap, wout.ap[:, k, dch * 128:(dch + 1) * 128], rhs, start=(k == 0), stop=(k == 7)),
                        r=[wout.res, fT.res, yt.res], w=[bk.res])
                P.op(DVE, lambda e, dch=dch, bk=bk: e.tensor_copy(o.ap[:, dch, :], bk.ap), r=[bk.res], w=[o.res])
                P.op(ACT, lambda e, dch=dch: e.activation(sqo.ap[:, dch, :], o.ap[:, dch, :], AF.Square), r=[o.res],
                     w=[sqo.res])
            self.epilogue(o, xres, 512, sqo, B[1], rstd, cf, 0)
            self.store(SP, o, x2T[:, tsl].rearrange("(k p) t -> p k t", p=128), o.ap)

        def s1(c):
            i = c % 2
            ct, bt, xs, zz, dt, hf, hb = ct_s[i], bt_s[i], xs_s[i], zz_s[i], dt_s[i], hf_s[i], hb_s[i]
            Ecs, seg, xdt = Ecs_s[i], seg_s[i], xdt_s[i]
            dtA, dtAb, negcs, cb, R = dtA_s[i], dtAb_s[i], negcs_s[i], cb_s[i], R_s[i]
            sl = slice(c * 128, (c + 1) * 128)
            self.load(SP, ct, ct.ap, CBT[0:512, sl].rearrange("(g p) t -> p g t", p=128))
            self.load(SP, bt, bt.ap, CBT[512:1024, sl].rearrange("(g p) t -> p g t", p=128))
            self.load(SP, xs, xs.ap, sc["XSx"][sl, :])
            self.load(SP, zz, zz.ap, sc["Zx"][sl, :])
            self.load(SP, dt, dt.ap, sc["DTx"][sl, :])
            self.load(SP, hf, hf.ap, sc["HS"][0, c])
            self.load(SP, hb, hb.ap, sc["HS"][1, c])
            P.op(DVE, lambda e: e.tensor_tensor(dtA.ap, dt.ap, self.avec.ap, ALU.mult), r=[dt.res, self.avec.res], w=[dtA.res])
            P.op(DVE, lambda e: e.tensor_copy(dtAb.ap, dtA.ap), r=[dtA.res], w=[dtAb.res])
            P.op(PE, lambda e: e.matmul(B[5].ap[:, 256:268], self.trif.ap, dtAb.ap[:, 0:12], start=True, stop=True),
                 r=[self.trif.res, dtAb.res], w=[B[5].res])
            P.op(PE, lambda e: e.matmul(B[5].ap[:, 268:280], self.trib.ap, dtAb.ap[:, 12:24], start=True, stop=True),
                 r=[self.trib.res, dtAb.res], w=[B[5].res])
            P.op(DVE, lambda e: e.tensor_scalar(negcs.ap, B[5].ap[:, 256:280], -1.0, None, ALU.mult), r=[B[5].res], w=[negcs.res])
            P.op(ACT, lambda e: e.activation(Ecs.ap, negcs.ap, AF.Exp, scale=-1.0), r=[negcs.res], w=[Ecs.res])
            for g in range(4):
                P.op(PE, lambda e, g=g: e.matmul(B[0].ap[:, g * 128:(g + 1) * 128], bt.ap[:, g, :], ct.ap[:, g, :], start=True,
                                                 stop=True), r=[bt.res, ct.res], w=[B[0].res])
            P.op(ACT, lambda e: e.activation(cb.ap, B[0].ap.rearrange("p (g l) -> p g l", g=4), AF.Identity), r=[B[0].res],
                 w=[cb.res])
            for d, tri in enumerate((self.trif, self.trib)):
                P.op(DVE, lambda e, d=d, tri=tri: e.tensor_tensor(
                    R.ap[:, d * 12:(d + 1) * 12, :], tri.ap.unsqueeze(1).to_broadcast([128, 12, 128]),
                    dtA.ap[:, d * 12:(d + 1) * 12].unsqueeze(2).to_broadcast([128, 12, 128]), ALU.mult),
                    r=[tri.res, dtA.res], w=[R.res])
            for q in range(6):
                bk = B[2 + q % 2]
                P.op(PE, lambda e, q=q, bk=bk: e.matmul(bk.ap, self.ones.ap, R.ap[:, 4 * q:4 * q + 4, :], start=True, stop=False),
                     r=[self.ones.res, R.res], w=[bk.res])
                nm = self.negm[q // 3]
                P.op(PE, lambda e, bk=bk, nm=nm: e.matmul(bk.ap, self.ident.ap, nm.ap, start=False, stop=True),
                     r=[self.ident.res, nm.res], w=[bk.res])
                for jj in range(4):
                    j = 4 * q + jj
                    P.op(ACT, lambda e, j=j, jj=jj, bk=bk: e.activation(seg.ap[:, j, :], bk.ap[:, jj * 128:(jj + 1) * 128], AF.Exp,
                                                                      bias=negcs.ap[:, j:j + 1], scale=1.0),
                         r=[bk.res, negcs.res], w=[seg.res])
            for d in range(2):
                for g in range(4):
                    j0 = d * 12 + g * 3
                    P.op(DVE, lambda e, j0=j0, g=g: e.tensor_tensor(
                        seg.ap[:, j0:j0 + 3, :], seg.ap[:, j0:j0 + 3, :], cb.ap[:, g:g + 1, :].to_broadcast([128, 3, 128]),
                        ALU.mult), r=[seg.res, cb.res], w=[seg.res])
                P.op(POOL, lambda e, d=d: e.tensor_tensor(h12(xdt.ap[:, d, :]), h12(xs.ap),
                                                         dt.ap[:, d * 12:(d + 1) * 12].unsqueeze(2).to_broadcast([128, 12, 64]),
                                                         ALU.mult), r=[xs.res, dt.res], w=[xdt.res])

        def s2(c):
            i = c % 2
            ct, bt, xs, zz, dt, hf, hb = ct_s[i], bt_s[i], xs_s[i], zz_s[i], dt_s[i], hf_s[i], hb_s[i]
            Ecs, seg, xdt = Ecs_s[i], seg_s[i], xdt_s[i]
            yacc, t1, sq, ss, rs, yn = yacc_s[i], t1_s[i], sq_s[i], ss_s[i], rs_s[i], yn_s[i]
            for h in range(12):
                bk, c0 = (B[4], h * 64) if h < 8 else (B[5], (h - 8) * 64)
                for d in range(2):
                    P.op(PE, lambda e, h=h, d=d, bk=bk, c0=c0: e.matmul(
                        bk.ap[:, c0:c0 + 64], seg.ap[:, d * 12 + h, :], xdt.ap[:, d, h * 64:(h + 1) * 64], start=(d == 0),
                        stop=(d == 1)), r=[seg.res, xdt.res], w=[bk.res])
            P.op(POOL, lambda e: e.tensor_tensor(h12(yacc.ap), h12(xs.ap), self.dsk.ap.unsqueeze(2).to_broadcast([128, 12, 64]),
                                                 ALU.mult), r=[xs.res, self.dsk.res], w=[yacc.res])
            P.op(DVE, lambda e: e.tensor_tensor(yacc.ap[:, 0:512], yacc.ap[:, 0:512], B[4].ap, ALU.add), r=[yacc.res, B[4].res],
                 w=[yacc.res])
            P.op(DVE, lambda e: e.tensor_tensor(yacc.ap[:, 512:768], yacc.ap[:, 512:768], B[5].ap[:, 0:256], ALU.add),
                 r=[yacc.res, B[5].res], w=[yacc.res])
            bks = (B[6], B[7])
            for d, hs in enumerate((hf, hb)):
                for g in range(4):
                    bk = bks[g // 2]
                    P.op(PE, lambda e, g=g, bk=bk, hs=hs: e.matmul(bk.ap[:, (g % 2) * 192:(g % 2 + 1) * 192], ct.ap[:, g, :],
                                                                 hs.ap[:, g * 192:(g + 1) * 192], start=True, stop=True),
                         r=[ct.res, hs.res], w=[bk.res])
                for hh in range(2):
                    P.op(DVE, lambda e, d=d, hh=hh, bk=bks[hh]: e.tensor_tensor(
                        t1.ap[:, hh * 384:(hh + 1) * 384].rearrange("p (h q) -> p h q", h=6),
                        bk.ap[:, 0:384].rearrange("p (h q) -> p h q", h=6),
                        Ecs.ap[:, d * 12 + hh * 6:d * 12 + hh * 6 + 6].unsqueeze(2).to_broadcast([128, 6, 64]), ALU.mult),
                        r=[bks[hh].res, Ecs.res], w=[t1.res])
                P.op(DVE, lambda e: e.tensor_tensor(yacc.ap, yacc.ap, t1.ap, ALU.add), r=[yacc.res, t1.res], w=[yacc.res])
            P.op(POOL, lambda e: e.tensor_tensor(yacc.ap, yacc.ap, zz.ap, ALU.mult), r=[yacc.res, zz.res], w=[yacc.res])
            P.op(POOL, lambda e: e.tensor_tensor(sq.ap, yacc.ap, yacc.ap, ALU.mult), r=[yacc.res], w=[sq.res])
            P.op(DVE, lambda e: e.tensor_reduce(ss.ap, sq.ap.rearrange("p (g q) -> p g q", g=4), mybir.AxisListType.X, ALU.add),
                 r=[sq.res], w=[ss.res])
            P.op(ACT, lambda e: e.activation(rs.ap, ss.ap, AF.Ln, bias=EPS, scale=1.0 / 192.0), r=[ss.res], w=[rs.res])
            P.op(ACT, lambda e: e.activation(rs.ap, rs.ap, AF.Exp, scale=-0.5), r=[rs.res], w=[rs.res])
            P.op(DVE, lambda e: e.tensor_tensor(t1.ap.rearrange("p (g q) -> p g q", g=4), yacc.ap.rearrange("p (g q) -> p g q", g=4),
                                                rs.ap.unsqueeze(2).to_broadcast([128, 4, 192]), ALU.mult),
                 r=[yacc.res, rs.res], w=[t1.res])
            P.op(POOL, lambda e: e.tensor_tensor(yn.ap, t1.ap, self.ssdg.ap, ALU.mult), r=[t1.res, self.ssdg.res], w=[yn.res])
            bkb = B[1].ap.bitcast(BF16)
            for q in range(6):
                P.op(PE, lambda e, q=q: e.transpose(bkb[:, q * 128:(q + 1) * 128], yn.ap[:, q * 128:(q + 1) * 128], self.ident.ap),
                     r=[yn.res, self.ident.res], w=[B[1].res])
            yt = YT[(c // 4) % 2]
            cc = c % 4
            P.op(ACT, lambda e: e.activation(yt.ap[:, :, cc * 128:(cc + 1) * 128], bkb[:, 0:768].rearrange("p (q l) -> p q l", q=6),
                                             AF.Identity), r=[B[1].res], w=[yt.res])
            if cc == 3:
                outproj(c // 4)

        s1(0)
        for c in range(32):
            if c + 1 < 32:
                s1(c + 1)
            s2(c)
        P.barrier()
        self.release(mark, [wout, o, xres] + ct_s + bt_s + xs_s + zz_s + dt_s + hf_s + hb_s)

    def finish(self):
        with self.nc.Block() as block:
            self.P.emit(block, self.psem)
        for cm in reversed(self.cms):
            cm.__exit__(None, None, None)
        return self.nc


def build(debug=(), nj=8, stop=99, skip_ffn1=False):
    b = Builder(debug)
    xT = b.din("xT", [D, L])
    ctxT = b.din("ctxT", [D, LC])
    w = {}
    for n in ("ffn1", "ffn2"):
        w[n] = (b.din(f"{n}_w_gate", [D, DFF]), b.din(f"{n}_w_up", [D, DFF]), b.din(f"{n}_w_down", [DFF, D]))
    w_in = b.din("w_in", [D, PROJ])
    w_out = b.din("w_out", [D, D])
    yT = b.dout("yT", [D, L])
    x1T = b.dscratch("x1T", [D, L])
    c1T = b.dscratch("c1T", [D, LC])
    x2T = b.dscratch("x2T", [D, L])
    sc = {}
    for nm, shp, dt in (("XSx", [L, 768], BF16), ("BSx", [L, 512], BF16), ("DTx", [L, 24], F32), ("CBT", [1024, L], BF16),
                        ("Zx", [L, 768], BF16), ("UF", [L, 256], BF16), ("XSc", [LC, 768], BF16), ("BSc", [LC, 512], BF16),
                        ("DTc", [LC, 24], F32), ("HS", [2, 32, 128, 768], BF16),
                        ("AS", [2, 64, 64, 256], BF16)):
        sc[nm] = b.dscratch(nm, shp, dt)
    b.sc = sc
    b.setup_consts()
    CONV1 = os.environ.get("K_CONV1", "1") == "1" and not skip_ffn1
    conv1 = []

    def bg1():
        o_, d_ = b.convert_weights_bycol("ffn1", *w["ffn1"])
        conv1.extend([o_, None, d_])
    b.phase_mod(bg1 if CONV1 else None)
    conv2 = b.convert_weights("ffn2", *w["ffn2"])
    if stop < 1:
        return b
    if skip_ffn1:
        x1T, c1T = xT, ctxT
    else:
        HMx = b.dscratch("HMx", [D, L], BF16)
        HMc = b.dscratch("HMc", [D, LC], BF16)
        jobs = [(xT[:, t * 512:(t + 1) * 512], x1T[:, t * 512:(t + 1) * 512], 512, 0, HMx[:, t * 512:(t + 1) * 512])
                for t in range(nj)]
        jobs.append((ctxT, c1T, LC, 1, HMc))
        if CONV1:
            b.phase_ffn("ffn1", jobs, *conv1[0], wdeps=conv1[2], background=conv2[1])
        else:
            b.phase_ffn("ffn1", jobs, *w["ffn1"], cast=True, background=conv2[1])
        conv2[1].clear()
    if stop < 2:
        return b
    for bgf in conv2[1]:
        bgf([])
    conv2[1].clear()
    b.setup_mix_consts()
    HMx_, HMc_ = (None, None) if skip_ffn1 else (HMx, HMc)
    b.phase_mix_in(c1T, LC, 1, w_in, dict(XS=sc["XSc"], BS=sc["BSc"], DT=sc["DTc"]), hm_src=HMc_)
    b.phase_mix_in(x1T, L, 0, w_in, dict(XS=sc["XSx"], BS=sc["BSx"], DT=sc["DTx"], CBT=sc["CBT"], Z=sc["Zx"], UF=sc["UF"]),
                   hm_src=HMx_)
    if stop < 3:
        return b
    MERGE = os.environ.get("K_MERGE", "1") == "1"
    if MERGE:
        b.fT = b.tile([128, 2, L], BF16, chan="hw")
        st_mark, st_tiles = b.phase_states(defer=True)
        b.phase_fourier(b.fT)
        mk_top = b.top
        b.release(st_mark, st_tiles)
        b.top = st_mark
    else:
        b.phase_states()
        if stop < 4:
            return b
        b.fT = b.tile([128, 2, L], BF16, chan="hw")
        b.phase_fourier(b.fT)
    if "fTd" in b.debug:
        fTd = b.dout("fTd", [2, 128, L], BF16)
        b.store(SP, b.fT, fTd.rearrange("c p k -> p c k"), b.fT.ap)
        b.P.barrier()
    if stop < 5:
        return b
    b.phase_ssd_out(x1T, x2T, w_out)
    if stop < 6:
        return b
    jobs = [(x2T[:, t * 512:(t + 1) * 512], yT[:, t * 512:(t + 1) * 512], 512, 0) for t in range(nj)]
    b.phase_ffn("ffn2", jobs, *conv2[0], wdeps=conv2[2])
    return b


def fourier_consts():
    l1 = np.arange(64)[:, None, None]
    l2 = np.arange(64)[None, :, None]
    k1 = np.arange(64)[None, None, :]
    ang = 2.0 * np.pi * ((64 * l1 + l2) * k1 % 4096) / 4096.0
    T1 = np.concatenate([np.cos(ang), -np.sin(ang)], axis=-1).reshape(64, 64 * 128)
    a = np.arange(64)
    ang64 = 2.0 * np.pi * (np.outer(a, a) % 64) / 64.0
    C, S = np.cos(ang64), np.sin(ang64)
    V3 = np.block([[C, -S], [S, C]])
    Z = np.zeros((64, 64))
    CSbd = np.stack([np.block([[C, Z], [Z, C]]), np.block([[S, Z], [Z, S]])])
    return {"T1": T1.astype(np.float32), "V3": V3.astype(np.float32), "CSbd": CSbd.astype(np.float32)}


def make_inputs(inputs, b):
    f = lambda a: np.ascontiguousarray(a, dtype=np.float32)
    x = inputs["x"]
    cvec = lambda v: np.asarray(v).reshape(8, 128).T
    norms = np.stack([cvec(inputs[n][0]) for n in ("ffn1_norm_pre", "ffn1_norm_post", "mix_norm_pre", "mix_norm_post",
                                                   "ffn2_norm_pre", "ffn2_norm_post")], axis=1)
    adab = np.asarray(inputs["ada_b"][0]).reshape(72, 128).T
    shared = {
        "ada_w": f(inputs["ada_w"][0]),
        "ada_b": f(adab),
        "norms": f(norms),
    }
    shared["w_in"] = f(inputs["w_in"][0])
    shared["w_out"] = f(inputs["w_out"][0])
    shared["conv_w"] = f(np.transpose(np.asarray(inputs["conv_w"][0]).reshape(3, 14, 128), (2, 1, 0)))
    shared["conv_b"] = f(np.asarray(inputs["conv_b"][0]).reshape(14, 128).T)
    rep = lambda v: np.broadcast_to(np.asarray(v).reshape(1, -1), (128, np.asarray(v).size))
    shared["dt_bias"] = f(rep(inputs["dt_bias"][0]))
    shared["a_log"] = f(rep(inputs["a_log"][0]))
    shared["d_skip"] = f(rep(inputs["d_skip"][0]))
    shared["ssd_norm"] = f(rep(inputs["ssd_norm"][0]))
    shared.update(fourier_consts())
    fw = np.asarray(inputs["fourier_w"][0])
    wbd = np.zeros((2, 128, 128), np.float32)
    for hd in range(4):
        o = (hd % 2) * 64
        wbd[hd // 2, o:o + 64, o:o + 64] = fw[hd]
    shared["fourier_wbd"] = wbd
    for n in ("ffn1", "ffn2"):
        shared[f"{n}_w_gate"] = f(inputs[f"{n}_w_gate"][0])
        shared[f"{n}_w_up"] = f(inputs[f"{n}_w_up"][0])
        shared[f"{n}_w_down"] = f(inputs[f"{n}_w_down"][0])
    maps = []
    for c in range(8):
        m = dict(shared)
        m["xT"] = f(np.asarray(x[c]).T)
        m["ctxT"] = f(np.asarray(inputs["ctx"][c]).T)
        cc = np.stack([cvec(inputs["c"][c]), cvec(inputs["c_ctx"])], axis=-1)
        m["cc"] = f(cc)
        maps.append(m)
    return maps


def kernel(**inputs):
    b = build()
    nc = b.finish()
    maps = make_inputs(inputs, b)
    maps = [{k: v for k, v in m.items() if k in b.ins} for m in maps]
    res = run_bass_kernel_spmd(nc, maps, core_ids=list(range(8)))
    out = np.stack([np.asarray(r["yT"]).T for r in res.results], axis=0)
    return np.ascontiguousarray(out, dtype=np.float32)
```

```python
import os
import numpy as np
import concourse.bass as bass
import concourse.mybir as mybir
from concourse.bass_utils import run_bass_kernel_spmd

F32 = mybir.dt.float32
BF16 = mybir.dt.bfloat16
AF = mybir.ActivationFunctionType
ALU = mybir.AluOpType

PE, ACT, DVE, POOL, SP = "tensor", "scalar", "vector", "gpsimd", "sync"
ENGS = (PE, ACT, DVE, POOL, SP)

D = 1024
L = 4096
LC = 256
DFF = 2816
NF = 22
PROJ = 2840
EPS = 1e-6


class Res:
    __slots__ = ("w", "rs")

    def __init__(self):
        self.w = None
        self.rs = {}


class Chan:
    __slots__ = ("sem", "count", "last", "bg")

    def __init__(self, sem):
        self.sem = sem
        self.count = 0
        self.last = None
        self.bg = False


class Ins:
    __slots__ = ("eng", "fn", "deps", "chan", "dval", "needed", "semval", "odeps", "n", "fin", "done", "barrier")

    def __init__(self, eng, fn, chan):
        self.eng = eng
        self.fn = fn
        self.chan = chan
        self.odeps = []
        self.n = 256
        self.fin = 0.0
        self.done = False
        self.barrier = False
        self.deps = []
        self.dval = None
        self.needed = False
        self.semval = None


class _Probe:
    def __getattr__(self, name):
        def f(*a, **k):
            out = k.get("out", a[0] if a else None)
            return out
        return f


def _free_elems(fn):
    try:
        out = fn(_Probe())
        shp = out.shape
        n = 1
        for d in shp[1:]:
            n *= int(d)
        return n, out.dtype
    except Exception:
        return None, None


class Prog:
    def __init__(self):
        self.q = {e: [] for e in ENGS}
        self.chans = []
        self.last_real = {}
        self.seq = []

    def chan(self, sem):
        c = Chan(sem)
        self.chans.append(c)
        return c

    def op(self, eng, fn, r=(), w=(), chan=None, n=None, after=()):
        ins = Ins(eng, fn, chan)
        if n is None:
            n, dt = _free_elems(fn)
            if n is None:
                n = 256
            elif chan is not None:
                n = n * (2 if dt == BF16 else 4)
        ins.n = n
        deps = list(after)
        isdma = chan is not None
        for res in r:
            d = res.w
            if d is not None and (isdma or d.chan is not None or d.eng != eng or eng != PE):
                deps.append(d)
        for res in w:
            d = res.w
            if d is not None:
                if isdma or d.chan is not None or d.eng != eng or eng != PE:
                    deps.append(d)
                else:
                    ins.odeps.append(d)
            for d in res.rs.values():
                if isdma or d.chan is not None or d.eng != eng or eng != PE:
                    deps.append(d)
        if isdma and chan.last is not None:
            ins.odeps.append(chan.last)
        seen = set()
        for d in deps:
            if id(d) not in seen and d is not ins:
                seen.add(id(d))
                ins.deps.append(d)
                d.needed = True
        for res in r:
            key = id(chan) if isdma else eng
            prev = res.rs.get(key)
            if prev is not None and prev is not ins:
                ins.odeps.append(prev)
            res.rs[key] = ins
        for res in w:
            res.w = ins
            res.rs = {}
        if isdma:
            chan.count += 16
            ins.dval = chan.count
            chan.last = ins
        self.seq.append(ins)
        return ins

    def barrier(self):
        self.seq.append(None)

    def schedule(self):
        segs, cur = [], []
        for ins in self.seq:
            if ins is None:
                segs.append(cur)
                cur = []
            else:
                cur.append(ins)
        if cur:
            segs.append(cur)
        self.q = {e: [] for e in ENGS}
        chan_last = {}
        for seg in segs:
            order = self.schedule_segment(seg)
            last = {}
            for ins in order:
                self.q[ins.eng].append(ins)
                if ins.chan is None:
                    last[ins.eng] = ins
                elif not ins.chan.bg:
                    chan_last[id(ins.chan)] = ins
            lasts = list(last.values()) + list(chan_last.values())
            for e in ENGS:
                b = Ins(e, lambda en: en.nop(nofuse=True), None)
                for d in lasts:
                    if d.chan is None and d.eng == e:
                        continue
                    b.deps.append(d)
                    d.needed = True
                self.q[e].append(b)

    @staticmethod
    def cost(ins):
        n = ins.n
        if ins.chan is not None:
            return 600.0, 2500.0 + n * 0.5
        if ins.eng == PE:
            return 60.0 + n * 0.33, None
        if ins.eng == POOL:
            return 1400.0 + n * 0.3, None
        if ins.eng == ACT:
            return 180.0 + n * 0.95, None
        return 110.0 + n * 0.8, None

    def schedule_segment(self, seg):
        W = int(os.environ.get("K_W", "80"))
        pend = {e: [] for e in ENGS}
        segset = set(seg)
        for ins in seg:
            ins.done = False
            pend[ins.eng].append(ins)
        head = {e: 0 for e in ENGS}
        free = {e: 0.0 for e in ENGS}
        order = []
        total = len(seg)
        while len(order) < total:
            best = None
            for e in ENGS:
                lst = pend[e]
                i = head[e]
                while i < len(lst) and lst[i].done:
                    i += 1
                head[e] = i
                cnt = 0
                while i < len(lst) and cnt < W:
                    ins = lst[i]
                    i += 1
                    if ins.done:
                        continue
                    cnt += 1
                    ok = True
                    t = free[e]
                    for d in ins.deps:
                        if not d.done:
                            if d in segset:
                                ok = False
                                break
                        elif d.fin > t:
                            t = d.fin
                    if not ok:
                        continue
                    for d in ins.odeps:
                        if not d.done and d in segset:
                            ok = False
                            break
                    if not ok:
                        continue
                    if best is None or t < best[0] - 1e-9:
                        best = (t, ins)
                    if t <= free[e] + 1e-9:
                        break
            t, ins = best
            occ, lat = self.cost(ins)
            free[ins.eng] = t + occ
            ins.fin = t + (lat if lat is not None else occ + float(os.environ.get('K_HOP', '800')))
            ins.done = True
            order.append(ins)
        return order

    def emit(self, block, psem):
        self.schedule()
        for e in ENGS:
            c = 0
            for ins in self.q[e]:
                if ins.chan is None and ins.needed:
                    c += 1
                    ins.semval = c
        prog = self

        def make(e):
            def body(en):
                known = {}
                for ins in prog.q[e]:
                    need = {}
                    for d in ins.deps:
                        if d.chan is not None:
                            sem, val = d.chan.sem, d.dval
                        else:
                            sem, val = psem[d.eng], d.semval
                        k = id(sem)
                        if known.get(k, 0) < val and need.get(k, (None, 0))[1] < val:
                            need[k] = (sem, val)
                    for k, (sem, val) in need.items():
                        en.wait_ge(sem, val)
                        known[k] = val
                    bi = ins.fn(en)
                    if ins.chan is not None:
                        bi.then_inc(ins.chan.sem, 16)
                    elif ins.needed:
                        bi.then_inc(psem[e], 1)
            return body

        for e in ENGS:
            if self.q[e]:
                getattr(block, e)(make(e))


class Tile:
    __slots__ = ("ap", "res", "chan", "kind")

    def __init__(self, ap, chan=None):
        self.kind = None
        self.ap = ap
        self.res = Res()
        self.chan = chan


ARENA_WORDS = 51800


class Builder:
    def __init__(self, debug=()):
        self.debug = set(debug)
        nc = bass.Bass("TRN2", target_bir_lowering=False)
        self.nc = nc
        self.P = Prog()
        self.cms = []

        def enter(cm):
            self.cms.append(cm)
            return cm.__enter__()
        self.arena = enter(nc.sbuf_tensor("arena", [128, ARENA_WORDS], F32))
        self.top = 0
        nsem = 96
        self.sems = [enter(nc.semaphore(f"s{i}")) for i in range(nsem)]
        self.psem = {e: self.sems[i] for i, e in enumerate(ENGS)}
        self.free_chans = {"hw": [self.P.chan(s) for s in self.sems[5:50]],
                           "sw": [self.P.chan(s) for s in self.sems[50:]]}
        self.banks = [Tile(enter(nc.psum_tensor(f"bank{i}", [128, 512], F32))[:, :]) for i in range(8)]
        self.ins = {}
        self.outs = {}

    def din(self, name, shape):
        ap = self.nc.dram_tensor(name, list(shape), F32, kind="ExternalInput").ap()
        self.ins[name] = ap
        return ap

    def dout(self, name, shape, dt=F32):
        ap = self.nc.dram_tensor(name, list(shape), dt, kind="ExternalOutput").ap()
        self.outs[name] = ap
        return ap

    def dscratch(self, name, shape, dt=F32):
        if name in self.debug:
            return self.dout(name, shape, dt)
        return self.nc.dram_tensor(name, list(shape), dt, kind="Internal").ap()

    def tile(self, shape, dt=F32, chan=False):
        n = int(np.prod(shape[1:]))
        nb = n * (2 if dt == BF16 else 4)
        n4 = (nb + 63) // 64 * 16
        o = self.top
        self.top += n4
        assert self.top <= ARENA_WORDS, f"SBUF arena overflow {self.top}"
        ap = self.arena[:, o:o + n4]
        if dt == BF16:
            ap = ap.bitcast(BF16)
        ap = ap[:, 0:n]
        if len(shape) == 3:
            ap = ap.rearrange("p (a b) -> p a b", a=shape[1])
        elif len(shape) == 4:
            ap = ap.rearrange("p (a b c) -> p a b c", a=shape[1], b=shape[2])
        t = Tile(ap, self.free_chans[chan].pop() if chan else None)
        t.kind = chan
        return t

    def release(self, mark, tiles):
        self.top = mark
        for t in tiles:
            if t.chan is not None:
                self.free_chans[t.kind].append(t.chan)
                t.chan = None

    def dma(self, eng, dst_ap, src_ap, chan, r=(), w=(), after=(), n=None):
        return self.P.op(eng, lambda e: e.dma_start(out=dst_ap, in_=src_ap), r=r, w=w, chan=chan, after=after, n=n)

    def load(self, eng, t, dst_ap, src_ap, after=(), n=None):
        return self.dma(eng, dst_ap, src_ap, t.chan, w=[t.res], after=after, n=n)

    def convert_weights(self, name, wg_d, wu_d, wd_d):
        outs, todo, done = [], [], []
        for nm, src, rows, cols in (("g", wg_d, D, DFF), ("u", wu_d, D, DFF), ("d", wd_d, DFF, D)):
            dst = self.nc.dram_tensor(f"{name}_w{nm}_bf", [rows, cols], BF16, kind="Internal").ap()
            ch = self.free_chans["sw"].pop()
            ch.bg = True
            for r0 in range(0, rows, 128):
                todo.append(lambda after, dst=dst, src=src, r0=r0, ch=ch, cols=cols: done.append(
                    self.dma(POOL, dst[r0:r0 + 128, :], src[r0:r0 + 128, :], ch, n=cols * 2, after=after)))
            outs.append(dst)
        return outs, todo, done

    def store(self, eng, t, dst_ap, src_ap):
        return self.dma(eng, dst_ap, src_ap, t.chan, r=[t.res])

    def convert_weights_bycol(self, name, wg_d, wu_d, wd_d):
        outs, deps = [], {}
        for nm, src, rows, cols, blk in (("g", wg_d, D, DFF, 256), ("u", wu_d, D, DFF, 256), ("d", wd_d, DFF, D, 128)):
            dst = self.nc.dram_tensor(f"{name}_w{nm}_bf", [rows, cols], BF16, kind="Internal").ap()
            outs.append(dst)
            deps[nm] = []
        order = []
        for g in range(11):
            order += [("g", g, 256), ("u", g, 256)]
        order += [("d", dch, 128) for dch in range(8)]
        srcs = {"g": wg_d, "u": wu_d, "d": wd_d}
        dsts = dict(zip("gud", outs))
        for nm, i, blk in order:
            ch = self.free_chans["sw"].pop()
            ch.bg = True
            ins = self.dma(POOL, dsts[nm][:, i * blk:(i + 1) * blk], srcs[nm][:, i * blk:(i + 1) * blk], ch, n=blk * 2 * 8)
            deps[nm].append(ins)
        return outs, deps

    def setup_consts(self):
        P = self.P
        self.ident = self.tile([128, 128], BF16)
        self.ones = self.tile([128, 128], BF16)
        tmp = self.tile([128, 128], F32)
        self.identf = tmp
        P.op(POOL, lambda e: e.memset(tmp.ap, 0.0), w=[tmp.res])
        P.op(POOL, lambda e: e.affine_select(tmp.ap, tmp.ap, [[-1, 128]], ALU.not_equal, 1.0, base=0,
                                             channel_multiplier=1), r=[tmp.res], w=[tmp.res])
        P.op(POOL, lambda e: e.tensor_copy(self.ident.ap, tmp.ap), r=[tmp.res], w=[self.ident.res])
        P.op(POOL, lambda e: e.memset(self.ones.ap, 1.0), w=[self.ones.res])

    def phase_mod(self, background=None):
        P = self.P
        cc_d = self.din("cc", [128, 8, 2])
        adaw = self.din("ada_w", [D, 9 * D])
        adab_d = self.din("ada_b", [128, 72])
        norms_d = self.din("norms", [128, 6, 8])
        self.mod = self.tile([128, 72, 2])
        self.norms = self.tile([128, 6, 8], chan="hw")
        self.coef = {}
        adab = self.tile([128, 72], chan="hw")
        ctiles = [(self.tile([128, 8, 2]), self.tile([128, 8, 2])) for _ in range(3)]
        mark = self.top
        cc = self.tile([128, 8, 2], chan="hw")
        scc = self.tile([128, 8, 2])
        wt = [self.tile([128, 8, 512], F32, chan="hw") for _ in range(3)]
        modrow = self.tile([128, 9 * D], F32)
        self.load(SP, cc, cc.ap, cc_d)
        self.load(SP, adab, adab.ap, adab_d)
        self.load(SP, self.norms, self.norms.ap, norms_d)
        P.op(ACT, lambda e: e.activation(scc.ap, cc.ap, AF.Silu), r=[cc.res], w=[scc.res])
        bank = self.banks[0]
        adaw_v = adaw.rearrange("(k p) n -> p k n", p=128)
        for g in range(18):
            t = wt[g % 3]
            self.load(SP, t, t.ap, adaw_v[:, :, g * 512:(g + 1) * 512])
            bk = self.banks[1 + g % 4]
            for k in range(8):
                P.op(PE, lambda e, t=t, k=k, bk=bk: e.matmul(bk.ap[0:2, :], scc.ap[:, k, :], t.ap[:, k, :], start=(k == 0),
                                                           stop=(k == 7)), r=[t.res, scc.res], w=[bk.res])
            if g % 2:
                P.op(ACT, lambda e, g=g, bk=bk: e.activation(modrow.ap[0:2, g * 512:(g + 1) * 512], bk.ap[0:2, :], AF.Identity),
                     r=[bk.res], w=[modrow.res])
            else:
                P.op(DVE, lambda e, g=g, bk=bk: e.tensor_copy(modrow.ap[0:2, g * 512:(g + 1) * 512], bk.ap[0:2, :]),
                     r=[bk.res], w=[modrow.res])
        for n in range(72):
            P.op(PE, lambda e, n=n: e.transpose(bank.ap[:, 2 * n:2 * n + 2], modrow.ap[0:2, n * 128:(n + 1) * 128],
                                                self.identf.ap[0:2, 0:2]), r=[modrow.res, self.identf.res], w=[bank.res])
        P.op(DVE, lambda e: e.tensor_tensor(self.mod.ap, bank.ap[:, 0:144].rearrange("p (n j) -> p n j", j=2),
                                            adab.ap.unsqueeze(2).to_broadcast([128, 72, 2]), ALU.add),
             r=[bank.res, adab.res], w=[self.mod.res])
        for si, (name, wgt) in enumerate((("ffn1", 0.5), ("mix", 1.0), ("ffn2", 0.5))):
            A, Bq = ctiles[si]
            j0 = 3 * si
            shift = self.mod.ap[:, j0 * 8:(j0 + 1) * 8, :]
            scale = self.mod.ap[:, (j0 + 1) * 8:(j0 + 2) * 8, :]
            gate = self.mod.ap[:, (j0 + 2) * 8:(j0 + 3) * 8, :]
            gpre = self.norms.ap[:, 2 * si, :].unsqueeze(2).to_broadcast([128, 8, 2])
            gpost = self.norms.ap[:, 2 * si + 1, :].unsqueeze(2).to_broadcast([128, 8, 2])
            P.op(DVE, lambda e, A=A, scale=scale, gpre=gpre: e.scalar_tensor_tensor(
                A.ap, scale, 1.0, gpre, ALU.add, ALU.mult), r=[self.mod.res, self.norms.res], w=[A.res])
            P.op(DVE, lambda e, Bq=Bq, gate=gate, gpost=gpost, wgt=wgt: e.scalar_tensor_tensor(
                Bq.ap, gate, wgt, gpost, ALU.mult, ALU.mult), r=[self.mod.res, self.norms.res], w=[Bq.res])
            self.coef[name] = dict(A=A, S=shift, B=Bq)
        if background is not None:
            background()
        P.barrier()
        self.release(mark, [cc] + wt)

    def prologue(self, x, T, sq, ssbank, rstd, tmp2, h, cf, j):
        P = self.P
        PRO = int(os.environ.get("K_PRO", "9"))
        if PRO < 1:
            return
        for k in range(8):
            if os.environ.get("K_SQ") == "dve":
                P.op(DVE, lambda e, k=k: e.tensor_tensor(sq.ap[:, k, :T], x.ap[:, k, :T], x.ap[:, k, :T], ALU.mult), r=[x.res], w=[sq.res])
            else:
                P.op(ACT, lambda e, k=k: e.activation(sq.ap[:, k, :T], x.ap[:, k, :T], AF.Square), r=[x.res], w=[sq.res])
        if PRO < 2:
            return
        for k in range(8):
            P.op(PE, lambda e, k=k: e.matmul(ssbank.ap[:, :T], self.ones.ap, sq.ap[:, k, :T], start=(k == 0),
                                             stop=(k == 7)), r=[sq.res, self.ones.res], w=[ssbank.res])
        if PRO < 3:
            return
        P.op(ACT, lambda e: e.activation(rstd.ap[:, :T], ssbank.ap[:, :T], AF.Ln, bias=EPS, scale=1.0 / D),
             r=[ssbank.res], w=[rstd.res])
        P.op(ACT, lambda e: e.activation(rstd.ap[:, :T], rstd.ap[:, :T], AF.Exp, scale=-0.5), r=[rstd.res], w=[rstd.res])
        if PRO < 4:
            return
        A, S = cf["A"], cf["S"]
        for k in range(8):
            t2 = tmp2[k % 2]
            if PRO != 6:
                P.op(DVE, lambda e, k=k, t2=t2: e.scalar_tensor_tensor(
                    t2.ap[:, :T], x.ap[:, k, :T], A.ap[:, k, j:j + 1], rstd.ap[:, :T], ALU.mult, ALU.mult),
                    r=[x.res, A.res, rstd.res], w=[t2.res])
            if PRO == 5:
                continue
            P.op(ACT, lambda e, k=k, t2=t2: e.activation(h.ap[:, k, :T], t2.ap[:, :T], AF.Identity,
                                                         bias=S[:, k, j:j + 1], scale=1.0),
                 r=[t2.res, self.mod.res], w=[h.res])

    def epilogue(self, y, xres, T, sqy, ssbank, rstd, cf, j):
        P = self.P
        for k in range(8):
            P.op(PE, lambda e, k=k: e.matmul(ssbank.ap[:, :T], self.ones.ap, sqy.ap[:, k, :T], start=(k == 0),
                                             stop=(k == 7)), r=[sqy.res, self.ones.res], w=[ssbank.res])
        P.op(ACT, lambda e: e.activation(rstd.ap[:, :T], ssbank.ap[:, :T], AF.Ln, bias=EPS, scale=1.0 / D),
             r=[ssbank.res], w=[rstd.res])
        P.op(ACT, lambda e: e.activation(rstd.ap[:, :T], rstd.ap[:, :T], AF.Exp, scale=-0.5), r=[rstd.res], w=[rstd.res])
        Bq = cf["B"]
        for k in range(8):
            P.op(DVE, lambda e, k=k: e.scalar_tensor_tensor(
                y.ap[:, k, :T], y.ap[:, k, :T], Bq.ap[:, k, j:j + 1], rstd.ap[:, :T], ALU.mult, ALU.mult),
                r=[y.res, Bq.res, rstd.res], w=[y.res])
            P.op(DVE, lambda e, k=k: e.tensor_tensor(
                y.ap[:, k, :T], y.ap[:, k, :T], xres.ap[:, k, :T], ALU.add), r=[y.res, xres.res], w=[y.res])

    def phase_ffn(self, name, jobs, wg_d, wu_d, wd_d, wdeps=(), cast=False, background=()):
        P = self.P
        cf = self.coef[name]
        mark = self.top
        xt = [self.tile([128, 8, 512], F32, chan="hw") for _ in range(2)]
        sqx = self.tile([128, 8, 512], BF16)
        sqy = self.tile([128, 8, 512], BF16)
        h = self.tile([128, 8, 512], BF16)
        a = self.tile([128, NF, 512], BF16)
        y = self.tile([128, 8, 512], F32, chan="hw")
        rstd = [self.tile([128, 512], F32) for _ in range(2)]
        tmp2 = [self.tile([128, 512], F32) for _ in range(2)]
        sg = [self.tile([128, 512], F32) for _ in range(2)]
        NGU, NWD = 4, 4
        wq, wk = (POOL, "sw") if cast else (SP, "hw")
        wgs = [self.tile([128, 8, 256], BF16, chan=wk) for _ in range(NGU)]
        wus = [self.tile([128, 8, 256], BF16, chan=wk) for _ in range(NGU)]
        wds = [self.tile([128, NF, 128], BF16, chan=wk) for _ in range(NWD)]
        background = list(background)
        per_tile = (len(background) + len(jobs) - 2) // max(1, len(jobs) - 1)
        alltiles = xt + [y] + wgs + wus + wds
        wg_v = wg_d.rearrange("(k p) f -> p k f", p=128)
        wu_v = wu_d.rearrange("(k p) f -> p k f", p=128)
        wd_v = wd_d.rearrange("(f p) d -> p f d", p=128)
        B = self.banks
        ssb_pro, ssb_epi = B[6], B[7]
        cnt = {"gu": 0, "wd": 0, "pair": 0, "yb": 0}

        jobs = [tuple(jb) + (None,) * (5 - len(jb)) for jb in jobs]
        h2 = self.tile([128, 8, 512], BF16, chan="hw")
        alltiles.append(h2)

        def load_x(i):
            src, dst, T, j, _ = jobs[i]
            t = xt[i % 2]
            self.load(SP, t, t.ap[:, :, :T], src.rearrange("(k p) t -> p k t", p=128))

        def pro(i):
            src, dst, T, j, _ = jobs[i]
            self.prologue(xt[0 if os.environ.get("K_X0") else i % 2], T, sqx, ssb_pro, rstd[0], tmp2, h, cf, j)

        CUT = int(os.environ.get("K_CUT", "9"))

        def do_tile(i, src, dst, T, j, hmdst):
            x = xt[i % 2]
            if i + 1 < len(jobs):
                load_x(i + 1)
            if CUT < 1:
                return
            for g in range(11):
                s = cnt["gu"] % NGU
                cnt["gu"] += 1
                wgt, wut = wgs[s], wus[s]
                dg = [wdeps["g"][g]] if isinstance(wdeps, dict) else wdeps
                du = [wdeps["u"][g]] if isinstance(wdeps, dict) else wdeps
                lw = self.load(wq, wgt, wgt.ap, wg_v[:, :, g * 256:(g + 1) * 256], after=dg)
                self.load(wq, wut, wut.ap, wu_v[:, :, g * 256:(g + 1) * 256], after=du)
                if g == 10:
                    for _ in range(per_tile):
                        if background:
                            background.pop(0)([lw])
                for c2 in range(2):
                    f = g * 2 + c2
                    pr = cnt["pair"] % 2
                    cnt["pair"] += 1
                    bg, bu = B[2 * pr], B[2 * pr + 1]
                    for k in range(8):
                        P.op(PE, lambda e, k=k, c2=c2, wgt=wgt, bg=bg: e.matmul(
                            bg.ap[:, :T], wgt.ap[:, k, c2 * 128:(c2 + 1) * 128], h.ap[:, k, :T], start=(k == 0),
                            stop=(k == 7)), r=[wgt.res, h.res], w=[bg.res])
                    for k in range(8):
                        P.op(PE, lambda e, k=k, c2=c2, wut=wut, bu=bu: e.matmul(
                            bu.ap[:, :T], wut.ap[:, k, c2 * 128:(c2 + 1) * 128], h.ap[:, k, :T], start=(k == 0),
                            stop=(k == 7)), r=[wut.res, h.res], w=[bu.res])
                    sgt = sg[pr]
                    P.op(ACT, lambda e, bg=bg, sgt=sgt: e.activation(sgt.ap[:, :T], bg.ap[:, :T], AF.Silu),
                         r=[bg.res], w=[sgt.res])
                    P.op(DVE, lambda e, bu=bu, sgt=sgt, f=f: e.tensor_tensor(a.ap[:, f, :T], sgt.ap[:, :T], bu.ap[:, :T],
                                                                            ALU.mult), r=[sgt.res, bu.res], w=[a.res])
            if CUT < 2:
                return
            if i + 1 < len(jobs) and not os.environ.get("K_NOSWP"):
                pro(i + 1)
            if CUT < 3:
                return
            for dch in range(8):
                s = cnt["wd"] % NWD
                cnt["wd"] += 1
                wdt = wds[s]
                dd_ = [wdeps["d"][dch]] if isinstance(wdeps, dict) else wdeps
                self.load(wq, wdt, wdt.ap, wd_v[:, :, dch * 128:(dch + 1) * 128], after=dd_)
                yb = B[4 + cnt["yb"] % 2]
                cnt["yb"] += 1
                DN = os.environ.get("K_DN", "mcs")
                if "m" in DN:
                    for f in range(NF):
                        P.op(PE, lambda e, f=f, wdt=wdt, yb=yb: e.matmul(
                            yb.ap[:, :T], wdt.ap[:, f, :], a.ap[:, f, :T], start=(f == 0), stop=(f == NF - 1)),
                            r=[wdt.res, a.res], w=[yb.res])
                if "c" in DN:
                    P.op(DVE, lambda e, dch=dch, yb=yb: e.tensor_copy(y.ap[:, dch, :T], yb.ap[:, :T]), r=[yb.res], w=[y.res])
                if "s" in DN:
                    P.op(ACT, lambda e, dch=dch: e.activation(sqy.ap[:, dch, :T], y.ap[:, dch, :T], AF.Square),
                         r=[y.res], w=[sqy.res])
            if CUT < 4:
                return
            self.epilogue(y, x, T, sqy, ssb_epi, rstd[1], cf, j)
            if CUT < 5:
                return
            self.store(SP, y, dst.rearrange("(k p) t -> p k t", p=128), y.ap[:, :, :T])
            if hmdst is not None:
                self.prologue(y, T, sqy, ssb_epi, rstd[1], tmp2, h2, self.coef["mix"], j)
                self.store(SP, h2, hmdst.rearrange("(k p) t -> p k t", p=128), h2.ap[:, :, :T])

        load_x(0)
        pro(0)
        for i, (src, dst, T, j, hmdst) in enumerate(jobs):
            do_tile(i, src, dst, T, j, hmdst)
        for bgf in background:
            bgf([])
        P.barrier()
        self.release(mark, alltiles)

    def view(self, t, ap):
        v = Tile(ap)
        v.res = t.res
        v.chan = t.chan
        v.kind = t.kind
        return v

    def setup_mix_consts(self):
        P = self.P
        nc = self.nc
        self.convw = self.tile([128, 14, 3], chan="hw")
        self.convb = self.tile([128, 14], chan="hw")
        self.dtb = self.tile([128, 24], chan="hw")
        self.avec = self.tile([128, 24], chan="hw")
        self.dsk = self.tile([128, 12], chan="hw")
        self.ssdg = self.tile([128, 768], chan="hw")
        for t, name, shp in ((self.convw, "conv_w", [128, 14, 3]), (self.convb, "conv_b", [128, 14]),
                             (self.dtb, "dt_bias", [128, 24]), (self.avec, "a_log", [128, 24]),
                             (self.dsk, "d_skip", [128, 12]), (self.ssdg, "ssd_norm", [128, 768])):
            self.load(SP, t, t.ap, self.din(name, shp))
        P.op(ACT, lambda e: e.activation(self.avec.ap, self.avec.ap, AF.Exp), r=[self.avec.res], w=[self.avec.res])
        P.op(DVE, lambda e: e.tensor_scalar(self.avec.ap, self.avec.ap, -1.0, None, ALU.mult), r=[self.avec.res],
             w=[self.avec.res])
        self.trif = self.tile([128, 128], BF16)
        self.trib = self.tile([128, 128], BF16)
        tmp = self.tile([128, 128], F32)
        for tri, pat, cm in ((self.trif, [[1, 128]], -1), (self.trib, [[-1, 128]], 1)):
            P.op(POOL, lambda e: e.memset(tmp.ap, 1.0), w=[tmp.res])
            P.op(POOL, lambda e, pat=pat, cm=cm: e.affine_select(tmp.ap, tmp.ap, pat, ALU.is_ge, 0.0, base=0,
                                                                 channel_multiplier=cm), r=[tmp.res], w=[tmp.res])
            P.op(POOL, lambda e, tri=tri: e.tensor_copy(tri.ap, tmp.ap), r=[tmp.res], w=[tri.res])
        self.negm = [self.tile([128, 4, 128], BF16) for _ in range(2)]
        tmp4 = self.tile([128, 4, 128], F32)
        for nm, pat, cm in ((self.negm[0], [[0, 4], [-1, 128]], 1), (self.negm[1], [[0, 4], [1, 128]], -1)):
            P.op(POOL, lambda e: e.memset(tmp4.ap, -30000.0), w=[tmp4.res])
            P.op(POOL, lambda e, pat=pat, cm=cm: e.affine_select(tmp4.ap, tmp4.ap, pat, ALU.is_ge, 0.0, base=-1,
                                                                 channel_multiplier=cm), r=[tmp4.res], w=[tmp4.res])
            P.op(POOL, lambda e, nm=nm: e.tensor_copy(nm.ap, tmp4.ap), r=[tmp4.res], w=[nm.res])
        self.HT = [self.tile([128, 768], F32) for _ in range(2)]
        for h in self.HT:
            P.op(POOL, lambda e, h=h: e.memset(h.ap, 0.0), w=[h.res])

    def phase_mix_in(self, srcT, Lq, j, w_in, outs, hm_src=None):
        P = self.P
        B = self.banks
        cf = self.coef["mix"]
        full = "Z" in outs
        T = min(512, Lq)
        nt = Lq // T
        nch = Lq // 128
        mark0 = self.top
        hm = self.tile([128, 8, Lq], BF16)
        if hm_src is not None:
            hm.chan = self.free_chans["hw"].pop()
            hm.kind = "hw"
            self.load(SP, hm, hm.ap, hm_src.rearrange("(k p) t -> p k t", p=128))
        else:
            mark = self.top
            xt = [self.tile([128, 8, 512], F32, chan="hw") for _ in range(2)]
            sqx = self.tile([128, 8, 512], BF16)
            rstd = self.tile([128, 512], F32)
            tmp2 = [self.tile([128, 512], F32) for _ in range(2)]
            for t in range(nt):
                x = xt[t % 2]
                self.load(SP, x, x.ap[:, :, :T], srcT[:, t * T:(t + 1) * T].rearrange("(k p) t -> p k t", p=128))
                self.prologue(x, T, sqx, B[6], rstd, tmp2, self.view(hm, hm.ap[:, :, t * T:(t + 1) * T]), cf, j)
            P.barrier()
            self.release(mark, xt)
        w_v = w_in.rearrange("(k p) n -> p k n", p=128)
        cl = list(range(14)) if full else list(range(4, 14))
        wsl = [self.tile([128, 8, 128], BF16, chan="sw") for _ in range(3)]
        urow = [self.tile([128, Lq + 2], F32) for _ in range(2)]
        acc = self.tile([128, Lq], F32)
        post = [self.tile([128, Lq], BF16, chan="hw") for _ in range(2)]
        tm = [self.tile([128, nch, 128], BF16, chan="hw") for _ in range(2)]
        for u in urow:
            P.op(POOL, lambda e, u=u: e.memset(u.ap, 0.0), w=[u.res])
        XSv = outs["XS"].rearrange("(c p) n -> p c n", p=128)
        BSv = outs["BS"].rearrange("(c p) n -> p c n", p=128)
        ntr = [0]

        def conv_chunk(idx, cc):
            ws = wsl[idx % 3]
            ur = urow[idx % 2]
            po = post[idx % 2]
            self.load(POOL, ws, ws.ap, w_v[:, :, 1024 + cc * 128:1024 + (cc + 1) * 128])
            for tt in range(nt):
                bk = B[tt % 2]
                for k in range(8):
                    P.op(PE, lambda e, k=k, tt=tt, bk=bk: e.matmul(bk.ap[:, :T], ws.ap[:, k, :], hm.ap[:, k, tt * T:(tt + 1) * T],
                                                               start=(k == 0), stop=(k == 7)), r=[ws.res, hm.res], w=[bk.res])
                if tt % 2:
                    P.op(ACT, lambda e, tt=tt, bk=bk: e.activation(ur.ap[:, 1 + tt * T:1 + (tt + 1) * T], bk.ap[:, :T],
                                                                   AF.Identity), r=[bk.res], w=[ur.res])
                else:
                    P.op(DVE, lambda e, tt=tt, bk=bk: e.tensor_copy(ur.ap[:, 1 + tt * T:1 + (tt + 1) * T], bk.ap[:, :T]),
                         r=[bk.res], w=[ur.res])
            cw = self.convw
            P.op(DVE, lambda e: e.tensor_scalar(acc.ap, ur.ap[:, 0:Lq], cw.ap[:, cc, 0:1], None, ALU.mult),
                 r=[ur.res, cw.res], w=[acc.res])
            P.op(DVE, lambda e: e.scalar_tensor_tensor(acc.ap, ur.ap[:, 1:Lq + 1], cw.ap[:, cc, 1:2], acc.ap, ALU.mult,
                                                        ALU.add), r=[ur.res, cw.res, acc.res], w=[acc.res])
            P.op(DVE, lambda e: e.scalar_tensor_tensor(acc.ap, ur.ap[:, 2:Lq + 2], cw.ap[:, cc, 2:3], acc.ap, ALU.mult,
                                                       ALU.add), r=[ur.res, cw.res, acc.res], w=[acc.res])
            P.op(ACT, lambda e: e.activation(po.ap, acc.ap, AF.Silu, bias=self.convb.ap[:, cc:cc + 1], scale=1.0),
                 r=[acc.res, self.convb.res], w=[po.res])
            if full and (cc < 4 or cc >= 10):
                row0 = cc * 128 if cc < 4 else 512 + (cc - 10) * 128
                self.store(SP, po, outs["CBT"][row0:row0 + 128, :], po.ap)
            if cc >= 4:
                tmt = tm[ntr[0] % 2]
                ntr[0] += 1
                for c0 in range(0, nch, 8):
                    n8 = min(8, nch - c0)
                    bk = B[2 + (c0 // 8) % 2]
                    bkb = bk.ap.bitcast(BF16)
                    for c in range(n8):
                        P.op(PE, lambda e, c=c, c0=c0, bkb=bkb: e.transpose(
                            bkb[:, c * 128:(c + 1) * 128], po.ap[:, (c0 + c) * 128:(c0 + c + 1) * 128], self.ident.ap),
                            r=[po.res, self.ident.res], w=[bk.res])
                    if (c0 // 8) % 2:
                        P.op(ACT, lambda e, c0=c0, n8=n8, bkb=bkb: e.activation(
                            tmt.ap[:, c0:c0 + n8, :], bkb[:, 0:n8 * 128].rearrange("p (c n) -> p c n", n=128), AF.Identity),
                            r=[bk.res], w=[tmt.res])
                    else:
                        P.op(DVE, lambda e, c0=c0, n8=n8, bkb=bkb: e.tensor_copy(
                            tmt.ap[:, c0:c0 + n8, :], bkb[:, 0:n8 * 128].rearrange("p (c n) -> p c n", n=128)),
                            r=[bk.res], w=[tmt.res])
                if cc < 10:
                    self.store(SP, tmt, XSv[:, :, (cc - 4) * 128:(cc - 3) * 128], tmt.ap)
                else:
                    self.store(SP, tmt, BSv[:, :, (cc - 10) * 128:(cc - 9) * 128], tmt.ap)

        for idx, cc in enumerate(cl):
            conv_chunk(idx, cc)
        wtok = self.tile([128, 8, 1048], BF16, chan="sw")
        if full:
            self.load(POOL, wtok, wtok.ap[:, :, 0:512], w_v[:, :, 0:512])
            self.load(POOL, wtok, wtok.ap[:, :, 512:1024], w_v[:, :, 512:1024])
        self.load(POOL, wtok, wtok.ap[:, :, 1024:1048], w_v[:, :, 2816:2840])
        ufst = [self.tile([128, 256], BF16, chan="hw") for _ in range(2)]
        zst = [self.tile([128, 768], BF16, chan="hw") for _ in range(2)]
        dtst = [self.tile([128, 24], F32, chan="hw") for _ in range(2)]
        spx2 = [[self.tile([128, 24], F32) for _ in range(3)] for _ in range(2)]

        def tok_chunk(c):
            uf, zs, ds = ufst[c % 2], zst[c % 2], dtst[c % 2]
            sl = slice(c * 128, (c + 1) * 128)
            bA, bB, bC = (B[4], B[5], B[7]) if c % 2 == 0 else (B[0], B[1], B[6])
            if full:
                for bk, c0 in ((bA, 0), (bB, 512)):
                    for k in range(8):
                        P.op(PE, lambda e, k=k, bk=bk, c0=c0: e.matmul(bk.ap, hm.ap[:, k, sl], wtok.ap[:, k, c0:c0 + 512],
                                                                     start=(k == 0), stop=(k == 7)),
                             r=[hm.res, wtok.res], w=[bk.res])
                P.op(ACT, lambda e: e.activation(uf.ap, bA.ap[:, 0:256], AF.Identity), r=[bA.res], w=[uf.res])
                P.op(ACT, lambda e: e.activation(zs.ap[:, 0:256], bA.ap[:, 256:512], AF.Silu), r=[bA.res], w=[zs.res])
                P.op(ACT, lambda e: e.activation(zs.ap[:, 256:768], bB.ap, AF.Silu), r=[bB.res], w=[zs.res])
                self.store(SP, uf, outs["UF"][sl, :], uf.ap)
                self.store(SP, zs, outs["Z"][sl, :], zs.ap)
            bk = bC
            for k in range(8):
                P.op(PE, lambda e, k=k: e.matmul(bk.ap[:, 0:24], hm.ap[:, k, sl], wtok.ap[:, k, 1024:1048], start=(k == 0),
                                                 stop=(k == 7)), r=[hm.res, wtok.res], w=[bk.res])
            P.op(DVE, lambda e: e.tensor_tensor(xball.ap[:, c, :], bk.ap[:, 0:24], self.dtb.ap, ALU.add),
                 r=[bk.res, self.dtb.res], w=[xball.res])

        xball = self.tile([128, nch, 24], F32)
        axall = self.tile([128, nch, 24], F32)
        dsall = self.tile([128, nch, 24], F32, chan="hw")
        for c in range(nch):
            tok_chunk(c)
        P.op(ACT, lambda e: e.activation(axall.ap, xball.ap, AF.Abs), r=[xball.res], w=[axall.res])
        P.op(ACT, lambda e: e.activation(axall.ap, axall.ap, AF.Exp, scale=-1.0), r=[axall.res], w=[axall.res])
        P.op(ACT, lambda e: e.activation(axall.ap, axall.ap, AF.Ln, bias=1.0, scale=1.0), r=[axall.res], w=[axall.res])
        P.op(DVE, lambda e: e.scalar_tensor_tensor(dsall.ap, xball.ap, 0.0, axall.ap, ALU.max, ALU.add),
             r=[xball.res, axall.res], w=[dsall.res])
        self.store(SP, dsall, outs["DT"].rearrange("(c p) n -> p c n", p=128), dsall.ap)
        P.barrier()
        self.release(mark0, post + tm + wsl + [wtok] + ufst + zst + dtst + [hm, dsall])

    def phase_states(self, defer=False):
        P, B, sc = self.P, self.banks, self.sc
        mark = self.top
        hsst = [self.tile([128, 768], BF16, chan="hw") for _ in range(2)]
        xs_s = [self.tile([128, 768], BF16, chan="hw") for _ in range(2)]
        bs_s = [self.tile([128, 512], BF16, chan="hw") for _ in range(2)]
        dt_s = [self.tile([128, 24], F32, chan="hw") for _ in range(2)]
        NB = 3
        dtab_s = [self.tile([128, 12], BF16) for _ in range(NB)]
        cst_s = [self.tile([128, 24], F32) for _ in range(NB)]
        dd_s = [self.tile([128, 12], F32) for _ in range(NB)]
        Wt_s = [self.tile([128, 12], F32) for _ in range(NB)]
        Td_s = [self.tile([128, 12], F32) for _ in range(NB)]
        coef_s = [self.tile([128, 12], F32) for _ in range(NB)]
        xw_s = [self.tile([128, 768], BF16) for _ in range(NB)]
        cnt = [0]

        def step(d, sq, c):
            i = cnt[0]
            cnt[0] += 1
            HT = self.HT[d]
            tri = self.trif if d == 0 else self.trib
            xs, bs, dt = xs_s[i % 2], bs_s[i % 2], dt_s[i % 2]
            dtab, cst, dd, Wt, Td, coef, xw = (t[i % NB] for t in (dtab_s, cst_s, dd_s, Wt_s, Td_s, coef_s, xw_s))
            Bs = B[(i % 2) * 3:(i % 2) * 3 + 3]
            sl = slice(c * 128, (c + 1) * 128)
            sfx = "c" if sq == "c" else "x"
            self.load(SP, xs, xs.ap, sc["XS" + sfx][sl, :])
            self.load(SP, bs, bs.ap, sc["BS" + sfx][sl, :])
            self.load(SP, dt, dt.ap, sc["DT" + sfx][sl, :])
            dsl = slice(d * 12, (d + 1) * 12)
            P.op(DVE, lambda e: e.tensor_tensor(dtab.ap, dt.ap[:, dsl], self.avec.ap[:, dsl], ALU.mult),
                 r=[dt.res, self.avec.res], w=[dtab.res])
            P.op(PE, lambda e: e.matmul(Bs[0].ap[:, 0:12], tri.ap, dtab.ap, start=True, stop=True), r=[tri.res, dtab.res],
                 w=[Bs[0].res])
            P.op(PE, lambda e: e.matmul(Bs[0].ap[:, 12:24], self.ones.ap, dtab.ap, start=True, stop=True),
                 r=[self.ones.res, dtab.res], w=[Bs[0].res])
            P.op(ACT, lambda e: e.activation(cst.ap, Bs[0].ap[:, 0:24], AF.Identity), r=[Bs[0].res], w=[cst.res])
            P.op(DVE, lambda e: e.tensor_tensor(dd.ap, cst.ap[:, 12:24], cst.ap[:, 0:12], ALU.subtract), r=[cst.res],
                 w=[dd.res])
            P.op(ACT, lambda e: e.activation(Wt.ap, dd.ap, AF.Exp), r=[dd.res], w=[Wt.res])
            P.op(ACT, lambda e: e.activation(Td.ap, cst.ap[:, 12:24], AF.Exp), r=[cst.res], w=[Td.res])
            P.op(DVE, lambda e: e.tensor_tensor(coef.ap, dt.ap[:, dsl], Wt.ap, ALU.mult), r=[dt.res, Wt.res], w=[coef.res])
            P.op(POOL, lambda e: e.tensor_tensor(xw.ap.rearrange("p (h q) -> p h q", h=12),
                                                 xs.ap.rearrange("p (h q) -> p h q", h=12),
                                                 coef.ap.unsqueeze(2).to_broadcast([128, 12, 64]), ALU.mult),
                 r=[xs.res, coef.res], w=[xw.res])
            if sq == "x":
                st = hsst[c % 2]
                P.op(ACT, lambda e: e.activation(st.ap, HT.ap, AF.Identity), r=[HT.res], w=[st.res])
                self.store(SP, st, sc["HS"][d, c], st.ap)
            for g in range(4):
                bk = Bs[1 + g // 2]
                P.op(PE, lambda e, g=g, bk=bk: e.matmul(bk.ap[:, (g % 2) * 192:(g % 2 + 1) * 192], bs.ap[:, g * 128:(g + 1) * 128],
                                                       xw.ap[:, g * 192:(g + 1) * 192], start=True, stop=True),
                     r=[bs.res, xw.res], w=[bk.res])
            P.op(DVE, lambda e: e.tensor_tensor(HT.ap.rearrange("p (h q) -> p h q", h=12),
                                                HT.ap.rearrange("p (h q) -> p h q", h=12),
                                                Td.ap.unsqueeze(2).to_broadcast([128, 12, 64]), ALU.mult),
                 r=[HT.res, Td.res], w=[HT.res])
            for hh in range(2):
                P.op(DVE, lambda e, hh=hh: e.tensor_tensor(HT.ap[:, hh * 384:(hh + 1) * 384], HT.ap[:, hh * 384:(hh + 1) * 384],
                                                         Bs[1 + hh].ap[:, 0:384], ALU.add), r=[HT.res, Bs[1 + hh].res],
                     w=[HT.res])

        for d in range(2):
            order = [("c", 0), ("c", 1)] + [("x", i) for i in range(32)]
            if d == 1:
                order = [("c", 1), ("c", 0)] + [("x", 31 - i) for i in range(32)]
            for sq, c in order:
                step(d, sq, c)
        if defer:
            return mark, hsst + xs_s + bs_s + dt_s
        P.barrier()
        self.release(mark, hsst + xs_s + bs_s + dt_s)

    def phase_fourier(self, fT):
        P, B, sc = self.P, self.banks, self.sc
        T1_d = self.din("T1", [64, 64 * 128])
        V3_d = self.din("V3", [128, 128])
        CS_d = self.din("CSbd", [2, 128, 128])
        W_d = self.din("fourier_wbd", [2, 128, 128])
        mark = self.top
        T1 = self.tile([128, 64, 128], BF16, chan="sw")
        V3 = self.tile([128, 128], BF16, chan="sw")
        CS = self.tile([128, 2, 128], BF16, chan="sw")
        Wbd = self.tile([128, 2, 128], BF16, chan="sw")
        BD = self.tile([128, 2, 2, 128], BF16)
        X1 = self.tile([128, 64, 256], BF16, chan="hw")
        A1 = self.tile([128, 64, 256], BF16, chan="hw")
        X3 = self.tile([128, 64, 256], BF16, chan="hw")
        PT = self.tile([128, 2, 2, 4096], BF16)
        self.load(POOL, T1, T1.ap[0:64], T1_d.rearrange("a (l k) -> a l k", k=128))
        self.load(POOL, V3, V3.ap, V3_d)
        self.load(POOL, CS, CS.ap, CS_d.rearrange("c p n -> p c n"))
        self.load(POOL, Wbd, Wbd.ap, W_d.rearrange("c p n -> p c n"))
        self.load(SP, X1, X1.ap[0:64], sc["UF"].rearrange("(a l) e -> a l e", l=64))
        for eb in range(2):
            for cs in range(2):
                bk = B[eb * 2 + cs]
                P.op(PE, lambda e, eb=eb, cs=cs, bk=bk: e.matmul(bk.ap[:, 0:128], CS.ap[:, cs, :], Wbd.ap[:, eb, :],
                                                               start=True, stop=True), r=[CS.res, Wbd.res], w=[bk.res])
                P.op(ACT, lambda e, eb=eb, cs=cs, bk=bk: e.activation(BD.ap[:, eb, cs, :], bk.ap[:, 0:128], AF.Identity,
                                                                      scale=1.0 / 512.0), r=[bk.res], w=[BD.res])
        for l2 in range(64):
            bk = B[4 + (l2 // 2) % 2]
            P.op(PE, lambda e, l2=l2, bk=bk: e.matmul(bk.ap[:, (l2 % 2) * 256:(l2 % 2 + 1) * 256], T1.ap[0:64, l2, :],
                                                    X1.ap[0:64, l2, :], start=True, stop=True), r=[T1.res, X1.res],
                 w=[bk.res])
            if l2 % 2 == 1:
                src = bk.ap.rearrange("p (a e) -> p a e", a=2)
                if (l2 // 2) % 2:
                    P.op(ACT, lambda e, l2=l2, src=src: e.activation(A1.ap[:, l2 - 1:l2 + 1, :], src, AF.Identity),
                         r=[bk.res], w=[A1.res])
                else:
                    P.op(DVE, lambda e, l2=l2, src=src: e.tensor_copy(A1.ap[:, l2 - 1:l2 + 1, :], src), r=[bk.res],
                         w=[A1.res])
        AS = sc["AS"]
        asr = Res()
        self.dma(SP, AS.rearrange("r k l e -> (r k) l e"), A1.ap, A1.chan, r=[A1.res], w=[asr])
        for r_ in range(2):
            self.P.op(SP, lambda e, r_=r_: e.dma_start(out=X3.ap[r_ * 64:(r_ + 1) * 64], in_=AS[r_].rearrange("k l e -> l k e")),
                      r=[asr], w=[X3.res], chan=X3.chan)
        grp = 0
        for eb in range(2):
            for k0 in range(0, 64, 4):
                bk = B[6 + grp % 2]
                for a in range(4):
                    P.op(PE, lambda e, eb=eb, k1=k0 + a, a=a, bk=bk: e.matmul(
                        bk.ap[:, a * 128:(a + 1) * 128], X3.ap[:, k1, eb * 128:(eb + 1) * 128], V3.ap, start=True, stop=True),
                        r=[X3.res, V3.res], w=[bk.res])
                for r_ in range(2):
                    src = bk.ap.rearrange("p (a r k) -> p r k a", a=4, r=2)[:, r_]
                    dst = PT.ap[:, eb, r_].rearrange("p (k a) -> p k a", a=64)[:, :, k0:k0 + 4]
                    if grp % 2:
                        P.op(ACT, lambda e, src=src, dst=dst: e.activation(dst, src, AF.Identity), r=[bk.res], w=[PT.res])
                    else:
                        P.op(DVE, lambda e, src=src, dst=dst: e.tensor_copy(dst, src), r=[bk.res], w=[PT.res])
                grp += 1
        for eb in range(2):
            for kt in range(8):
                bk = B[(eb * 8 + kt) % 4]
                for cs in range(2):
                    P.op(PE, lambda e, eb=eb, kt=kt, cs=cs, bk=bk: e.matmul(
                        bk.ap, BD.ap[:, eb, cs, :], PT.ap[:, eb, cs, kt * 512:(kt + 1) * 512], start=(cs == 0), stop=(cs == 1)),
                        r=[BD.res, PT.res], w=[bk.res])
                if kt % 2:
                    P.op(ACT, lambda e, eb=eb, kt=kt, bk=bk: e.activation(fT.ap[:, eb, kt * 512:(kt + 1) * 512], bk.ap,
                                                                        AF.Identity), r=[bk.res], w=[fT.res])
                else:
                    P.op(DVE, lambda e, eb=eb, kt=kt, bk=bk: e.tensor_copy(fT.ap[:, eb, kt * 512:(kt + 1) * 512], bk.ap),
                         r=[bk.res], w=[fT.res])
        P.barrier()
        self.release(mark, [T1, V3, CS, Wbd, X1, A1, X3])

    def phase_ssd_out(self, x1T, x2T, w_out_d):
        P, B, sc = self.P, self.banks, self.sc
        cf = self.coef["mix"]
        fT = self.fT
        mark = self.top
        wout = self.tile([128, 8, 1024], BF16, chan="sw")
        wo_v = w_out_d.rearrange("(k p) n -> p k n", p=128)
        for hh in range(2):
            self.load(POOL, wout, wout.ap[:, :, hh * 512:(hh + 1) * 512], wo_v[:, :, hh * 512:(hh + 1) * 512])
        mk = lambda shp, dt=F32, chan=None: [self.tile(shp, dt, chan=chan) for _ in range(2)]
        ct_s, bt_s = mk([128, 4, 128], BF16, "hw"), mk([128, 4, 128], BF16, "hw")
        xs_s, zz_s = mk([128, 768], BF16, "hw"), mk([128, 768], BF16, "hw")
        dt_s = mk([128, 24], F32, "hw")
        hf_s, hb_s = mk([128, 768], BF16, "hw"), mk([128, 768], BF16, "hw")
        dtA_s, dtAb_s, negcs_s = mk([128, 24]), mk([128, 24], BF16), mk([128, 24])
        Ecs_s = mk([128, 24])
        cb_s = mk([128, 4, 128], BF16)
        R_s = mk([128, 24, 128], BF16)
        seg_s = mk([128, 24, 128], BF16)
        xdt_s = mk([128, 2, 768], BF16)
        yacc_s, t1_s, sq_s = mk([128, 768]), mk([128, 768]), mk([128, 768])
        ss_s, rs_s = mk([128, 4]), mk([128, 4])
        yn_s = mk([128, 768], BF16)
        YT = mk([128, 6, 512], BF16)
        o = self.tile([128, 8, 512], F32, chan="hw")
        sqo = self.tile([128, 8, 512], BF16)
        xres = self.tile([128, 8, 512], F32, chan="hw")
        rstd = self.tile([128, 512])
        CBT = sc["CBT"]
        h12 = lambda ap: ap.rearrange("p (h q) -> p h q", h=12)

        def outproj(tt):
            yt = YT[tt % 2]
            tsl = slice(tt * 512, (tt + 1) * 512)
            self.load(SP, xres, xres.ap, x1T[:, tsl].rearrange("(k p) t -> p k t", p=128))
            for dch in range(8):
                bk = B[6 + dch % 2]
                for k in range(8):
                    rhs = fT.ap[:, k, tsl] if k < 2 else yt.ap[:, k - 2, :]
                    P.op(PE, lambda e, k=k, dch=dch, bk=bk, rhs=rhs: e.matmul(
                        bk.ap, wout.ap[:, k, dch * 128:(dch + 1) * 128], rhs, start=(k == 0), stop=(k == 7)),
                        r=[wout.res, fT.res, yt.res], w=[bk.res])
                P.op(DVE, lambda e, dch=dch, bk=bk: e.tensor_copy(o.ap[:, dch, :], bk.ap), r=[bk.res], w=[o.res])
                P.op(ACT, lambda e, dch=dch: e.activation(sqo.ap[:, dch, :], o.ap[:, dch, :], AF.Square), r=[o.res],
                     w=[sqo.res])
            self.epilogue(o, xres, 512, sqo, B[1], rstd, cf, 0)
            self.store(SP, o, x2T[:, tsl].rearrange("(k p) t -> p k t", p=128), o.ap)

        def s1(c):
            i = c % 2
            ct, bt, xs, zz, dt, hf, hb = ct_s[i], bt_s[i], xs_s[i], zz_s[i], dt_s[i], hf_s[i], hb_s[i]
            Ecs, seg, xdt = Ecs_s[i], seg_s[i], xdt_s[i]
            dtA, dtAb, negcs, cb, R = dtA_s[i], dtAb_s[i], negcs_s[i], cb_s[i], R_s[i]
            sl = slice(c * 128, (c + 1) * 128)
            self.load(SP, ct, ct.ap, CBT[0:512, sl].rearrange("(g p) t -> p g t", p=128))
            self.load(SP, bt, bt.ap, CBT[512:1024, sl].rearrange("(g p) t -> p g t", p=128))
            self.load(SP, xs, xs.ap, sc["XSx"][sl, :])
            self.load(SP, zz, zz.ap, sc["Zx"][sl, :])
            self.load(SP, dt, dt.ap, sc["DTx"][sl, :])
            self.load(SP, hf, hf.ap, sc["HS"][0, c])
            self.load(SP, hb, hb.ap, sc["HS"][1, c])
            P.op(DVE, lambda e: e.tensor_tensor(dtA.ap, dt.ap, self.avec.ap, ALU.mult), r=[dt.res, self.avec.res], w=[dtA.res])
            P.op(DVE, lambda e: e.tensor_copy(dtAb.ap, dtA.ap), r=[dtA.res], w=[dtAb.res])
            P.op(PE, lambda e: e.matmul(B[5].ap[:, 256:268], self.trif.ap, dtAb.ap[:, 0:12], start=True, stop=True),
                 r=[self.trif.res, dtAb.res], w=[B[5].res])
            P.op(PE, lambda e: e.matmul(B[5].ap[:, 268:280], self.trib.ap, dtAb.ap[:, 12:24], start=True, stop=True),
                 r=[self.trib.res, dtAb.res], w=[B[5].res])
            P.op(DVE, lambda e: e.tensor_scalar(negcs.ap, B[5].ap[:, 256:280], -1.0, None, ALU.mult), r=[B[5].res], w=[negcs.res])
            P.op(ACT, lambda e: e.activation(Ecs.ap, negcs.ap, AF.Exp, scale=-1.0), r=[negcs.res], w=[Ecs.res])
            for g in range(4):
                P.op(PE, lambda e, g=g: e.matmul(B[0].ap[:, g * 128:(g + 1) * 128], bt.ap[:, g, :], ct.ap[:, g, :], start=True,
                                                 stop=True), r=[bt.res, ct.res], w=[B[0].res])
            P.op(ACT, lambda e: e.activation(cb.ap, B[0].ap.rearrange("p (g l) -> p g l", g=4), AF.Identity), r=[B[0].res],
                 w=[cb.res])
            for d, tri in enumerate((self.trif, self.trib)):
                P.op(DVE, lambda e, d=d, tri=tri: e.tensor_tensor(
                    R.ap[:, d * 12:(d + 1) * 12, :], tri.ap.unsqueeze(1).to_broadcast([128, 12, 128]),
                    dtA.ap[:, d * 12:(d + 1) * 12].unsqueeze(2).to_broadcast([128, 12, 128]), ALU.mult),
                    r=[tri.res, dtA.res], w=[R.res])
            for q in range(6):
                bk = B[2 + q % 2]
                P.op(PE, lambda e, q=q, bk=bk: e.matmul(bk.ap, self.ones.ap, R.ap[:, 4 * q:4 * q + 4, :], start=True, stop=False),
                     r=[self.ones.res, R.res], w=[bk.res])
                nm = self.negm[q // 3]
                P.op(PE, lambda e, bk=bk, nm=nm: e.matmul(bk.ap, self.ident.ap, nm.ap, start=False, stop=True),
                     r=[self.ident.res, nm.res], w=[bk.res])
                for jj in range(4):
                    j = 4 * q + jj
                    P.op(ACT, lambda e, j=j, jj=jj, bk=bk: e.activation(seg.ap[:, j, :], bk.ap[:, jj * 128:(jj + 1) * 128], AF.Exp,
                                                                      bias=negcs.ap[:, j:j + 1], scale=1.0),
                         r=[bk.res, negcs.res], w=[seg.res])
            for d in range(2):
                for g in range(4):
                    j0 = d * 12 + g * 3
                    P.op(DVE, lambda e, j0=j0, g=g: e.tensor_tensor(
                        seg.ap[:, j0:j0 + 3, :], seg.ap[:, j0:j0 + 3, :], cb.ap[:, g:g + 1, :].to_broadcast([128, 3, 128]),
                        ALU.mult), r=[seg.res, cb.res], w=[seg.res])
                P.op(POOL, lambda e, d=d: e.tensor_tensor(h12(xdt.ap[:, d, :]), h12(xs.ap),
                                                         dt.ap[:, d * 12:(d + 1) * 12].unsqueeze(2).to_broadcast([128, 12, 64]),
                                                         ALU.mult), r=[xs.res, dt.res], w=[xdt.res])

        def s2(c):
            i = c % 2
            ct, bt, xs, zz, dt, hf, hb = ct_s[i], bt_s[i], xs_s[i], zz_s[i], dt_s[i], hf_s[i], hb_s[i]
            Ecs, seg, xdt = Ecs_s[i], seg_s[i], xdt_s[i]
            yacc, t1, sq, ss, rs, yn = yacc_s[i], t1_s[i], sq_s[i], ss_s[i], rs_s[i], yn_s[i]
            for h in range(12):
                bk, c0 = (B[4], h * 64) if h < 8 else (B[5], (h - 8) * 64)
                for d in range(2):
                    P.op(PE, lambda e, h=h, d=d, bk=bk, c0=c0: e.matmul(
                        bk.ap[:, c0:c0 + 64], seg.ap[:, d * 12 + h, :], xdt.ap[:, d, h * 64:(h + 1) * 64], start=(d == 0),
                        stop=(d == 1)), r=[seg.res, xdt.res], w=[bk.res])
            P.op(POOL, lambda e: e.tensor_tensor(h12(yacc.ap), h12(xs.ap), self.dsk.ap.unsqueeze(2).to_broadcast([128, 12, 64]),
                                                 ALU.mult), r=[xs.res, self.dsk.res], w=[yacc.res])
            P.op(DVE, lambda e: e.tensor_tensor(yacc.ap[:, 0:512], yacc.ap[:, 0:512], B[4].ap, ALU.add), r=[yacc.res, B[4].res],
                 w=[yacc.res])
            P.op(DVE, lambda e: e.tensor_tensor(yacc.ap[:, 512:768], yacc.ap[:, 512:768], B[5].ap[:, 0:256], ALU.add),
                 r=[yacc.res, B[5].res], w=[yacc.res])
            bks = (B[6], B[7])
            for d, hs in enumerate((hf, hb)):
                for g in range(4):
                    bk = bks[g // 2]
                    P.op(PE, lambda e, g=g, bk=bk, hs=hs: e.matmul(bk.ap[:, (g % 2) * 192:(g % 2 + 1) * 192], ct.ap[:, g, :],
                                                                 hs.ap[:, g * 192:(g + 1) * 192], start=True, stop=True),
                         r=[ct.res, hs.res], w=[bk.res])
                for hh in range(2):
                    P.op(DVE, lambda e, d=d, hh=hh, bk=bks[hh]: e.tensor_tensor(
                        t1.ap[:, hh * 384:(hh + 1) * 384].rearrange("p (h q) -> p h q", h=6),
                        bk.ap[:, 0:384].rearrange("p (h q) -> p h q", h=6),
                        Ecs.ap[:, d * 12 + hh * 6:d * 12 + hh * 6 + 6].unsqueeze(2).to_broadcast([128, 6, 64]), ALU.mult),
                        r=[bks[hh].res, Ecs.res], w=[t1.res])
                P.op(DVE, lambda e: e.tensor_tensor(yacc.ap, yacc.ap, t1.ap, ALU.add), r=[yacc.res, t1.res], w=[yacc.res])
            P.op(POOL, lambda e: e.tensor_tensor(yacc.ap, yacc.ap, zz.ap, ALU.mult), r=[yacc.res, zz.res], w=[yacc.res])
            P.op(POOL, lambda e: e.tensor_tensor(sq.ap, yacc.ap, yacc.ap, ALU.mult), r=[yacc.res], w=[sq.res])
            P.op(DVE, lambda e: e.tensor_reduce(ss.ap, sq.ap.rearrange("p (g q) -> p g q", g=4), mybir.AxisListType.X, ALU.add),
                 r=[sq.res], w=[ss.res])
            P.op(ACT, lambda e: e.activation(rs.ap, ss.ap, AF.Ln, bias=EPS, scale=1.0 / 192.0), r=[ss.res], w=[rs.res])
            P.op(ACT, lambda e: e.activation(rs.ap, rs.ap, AF.Exp, scale=-0.5), r=[rs.res], w=[rs.res])
            P.op(DVE, lambda e: e.tensor_tensor(t1.ap.rearrange("p (g q) -> p g q", g=4), yacc.ap.rearrange("p (g q) -> p g q", g=4),
                                                rs.ap.unsqueeze(2).to_broadcast([128, 4, 192]), ALU.mult),
                 r=[yacc.res, rs.res], w=[t1.res])
            P.op(POOL, lambda e: e.tensor_tensor(yn.ap, t1.ap, self.ssdg.ap, ALU.mult), r=[t1.res, self.ssdg.res], w=[yn.res])
            bkb = B[1].ap.bitcast(BF16)
            for q in range(6):
                P.op(PE, lambda e, q=q: e.transpose(bkb[:, q * 128:(q + 1) * 128], yn.ap[:, q * 128:(q + 1) * 128], self.ident.ap),
                     r=[yn.res, self.ident.res], w=[B[1].res])
            yt = YT[(c // 4) % 2]
            cc = c % 4
            P.op(ACT, lambda e: e.activation(yt.ap[:, :, cc * 128:(cc + 1) * 128], bkb[:, 0:768].rearrange("p (q l) -> p q l", q=6),
                                             AF.Identity), r=[B[1].res], w=[yt.res])
            if cc == 3:
                outproj(c // 4)

        s1(0)
        for c in range(32):
            if c + 1 < 32:
                s1(c + 1)
            s2(c)
        P.barrier()
        self.release(mark, [wout, o, xres] + ct_s + bt_s + xs_s + zz_s + dt_s + hf_s + hb_s)

    def finish(self):
        with self.nc.Block() as block:
            self.P.emit(block, self.psem)
        for cm in reversed(self.cms):
            cm.__exit__(None, None, None)
        return self.nc


def build(debug=(), nj=8, stop=99, skip_ffn1=False):
    b = Builder(debug)
    xT = b.din("xT", [D, L])
    ctxT = b.din("ctxT", [D, LC])
    w = {}
    for n in ("ffn1", "ffn2"):
        w[n] = (b.din(f"{n}_w_gate", [D, DFF]), b.din(f"{n}_w_up", [D, DFF]), b.din(f"{n}_w_down", [DFF, D]))
    w_in = b.din("w_in", [D, PROJ])
    w_out = b.din("w_out", [D, D])
    yT = b.dout("yT", [D, L])
    x1T = b.dscratch("x1T", [D, L])
    c1T = b.dscratch("c1T", [D, LC])
    x2T = b.dscratch("x2T", [D, L])
    sc = {}
    for nm, shp, dt in (("XSx", [L, 768], BF16), ("BSx", [L, 512], BF16), ("DTx", [L, 24], F32), ("CBT", [1024, L], BF16),
                        ("Zx", [L, 768], BF16), ("UF", [L, 256], BF16), ("XSc", [LC, 768], BF16), ("BSc", [LC, 512], BF16),
                        ("DTc", [LC, 24], F32), ("HS", [2, 32, 128, 768], BF16),
                        ("AS", [2, 64, 64, 256], BF16)):
        sc[nm] = b.dscratch(nm, shp, dt)
    b.sc = sc
    b.setup_consts()
    CONV1 = os.environ.get("K_CONV1", "1") == "1" and not skip_ffn1
    conv1 = []

    def bg1():
        o_, d_ = b.convert_weights_bycol("ffn1", *w["ffn1"])
        conv1.extend([o_, None, d_])
    b.phase_mod(bg1 if CONV1 else None)
    conv2 = b.convert_weights("ffn2", *w["ffn2"])
    if stop < 1:
        return b
    if skip_ffn1:
        x1T, c1T = xT, ctxT
    else:
        HMx = b.dscratch("HMx", [D, L], BF16)
        HMc = b.dscratch("HMc", [D, LC], BF16)
        jobs = [(xT[:, t * 512:(t + 1) * 512], x1T[:, t * 512:(t + 1) * 512], 512, 0, HMx[:, t * 512:(t + 1) * 512])
                for t in range(nj)]
        jobs.append((ctxT, c1T, LC, 1, HMc))
        if CONV1:
            b.phase_ffn("ffn1", jobs, *conv1[0], wdeps=conv1[2], background=conv2[1])
        else:
            b.phase_ffn("ffn1", jobs, *w["ffn1"], cast=True, background=conv2[1])
        conv2[1].clear()
    if stop < 2:
        return b
    for bgf in conv2[1]:
        bgf([])
    conv2[1].clear()
    b.setup_mix_consts()
    HMx_, HMc_ = (None, None) if skip_ffn1 else (HMx, HMc)
    b.phase_mix_in(c1T, LC, 1, w_in, dict(XS=sc["XSc"], BS=sc["BSc"], DT=sc["DTc"]), hm_src=HMc_)
    b.phase_mix_in(x1T, L, 0, w_in, dict(XS=sc["XSx"], BS=sc["BSx"], DT=sc["DTx"], CBT=sc["CBT"], Z=sc["Zx"], UF=sc["UF"]),
                   hm_src=HMx_)
    if stop < 3:
        return b
    MERGE = os.environ.get("K_MERGE", "1") == "1"
    if MERGE:
        b.fT = b.tile([128, 2, L], BF16, chan="hw")
        st_mark, st_tiles = b.phase_states(defer=True)
        b.phase_fourier(b.fT)
        mk_top = b.top
        b.release(st_mark, st_tiles)
        b.top = st_mark
    else:
        b.phase_states()
        if stop < 4:
            return b
        b.fT = b.tile([128, 2, L], BF16, chan="hw")
        b.phase_fourier(b.fT)
    if "fTd" in b.debug:
        fTd = b.dout("fTd", [2, 128, L], BF16)
        b.store(SP, b.fT, fTd.rearrange("c p k -> p c k"), b.fT.ap)
        b.P.barrier()
    if stop < 5:
        return b
    b.phase_ssd_out(x1T, x2T, w_out)
    if stop < 6:
        return b
    jobs = [(x2T[:, t * 512:(t + 1) * 512], yT[:, t * 512:(t + 1) * 512], 512, 0) for t in range(nj)]
    b.phase_ffn("ffn2", jobs, *conv2[0], wdeps=conv2[2])
    return b


def fourier_consts():
    l1 = np.arange(64)[:, None, None]
    l2 = np.arange(64)[None, :, None]
    k1 = np.arange(64)[None, None, :]
    ang = 2.0 * np.pi * ((64 * l1 + l2) * k1 % 4096) / 4096.0
    T1 = np.concatenate([np.cos(ang), -np.sin(ang)], axis=-1).reshape(64, 64 * 128)
    a = np.arange(64)
    ang64 = 2.0 * np.pi * (np.outer(a, a) % 64) / 64.0
    C, S = np.cos(ang64), np.sin(ang64)
    V3 = np.block([[C, -S], [S, C]])
    Z = np.zeros((64, 64))
    CSbd = np.stack([np.block([[C, Z], [Z, C]]), np.block([[S, Z], [Z, S]])])
    return {"T1": T1.astype(np.float32), "V3": V3.astype(np.float32), "CSbd": CSbd.astype(np.float32)}


def make_inputs(inputs, b):
    f = lambda a: np.ascontiguousarray(a, dtype=np.float32)
    x = inputs["x"]
    cvec = lambda v: np.asarray(v).reshape(8, 128).T
    norms = np.stack([cvec(inputs[n][0]) for n in ("ffn1_norm_pre", "ffn1_norm_post", "mix_norm_pre", "mix_norm_post",
                                                   "ffn2_norm_pre", "ffn2_norm_post")], axis=1)
    adab = np.asarray(inputs["ada_b"][0]).reshape(72, 128).T
    shared = {
        "ada_w": f(inputs["ada_w"][0]),
        "ada_b": f(adab),
        "norms": f(norms),
    }
    shared["w_in"] = f(inputs["w_in"][0])
    shared["w_out"] = f(inputs["w_out"][0])
    shared["conv_w"] = f(np.transpose(np.asarray(inputs["conv_w"][0]).reshape(3, 14, 128), (2, 1, 0)))
    shared["conv_b"] = f(np.asarray(inputs["conv_b"][0]).reshape(14, 128).T)
    rep = lambda v: np.broadcast_to(np.asarray(v).reshape(1, -1), (128, np.asarray(v).size))
    shared["dt_bias"] = f(rep(inputs["dt_bias"][0]))
    shared["a_log"] = f(rep(inputs["a_log"][0]))
    shared["d_skip"] = f(rep(inputs["d_skip"][0]))
    shared["ssd_norm"] = f(rep(inputs["ssd_norm"][0]))
    shared.update(fourier_consts())
    fw = np.asarray(inputs["fourier_w"][0])
    wbd = np.zeros((2, 128, 128), np.float32)
    for hd in range(4):
        o = (hd % 2) * 64
        wbd[hd // 2, o:o + 64, o:o + 64] = fw[hd]
    shared["fourier_wbd"] = wbd
    for n in ("ffn1", "ffn2"):
        shared[f"{n}_w_gate"] = f(inputs[f"{n}_w_gate"][0])
        shared[f"{n}_w_up"] = f(inputs[f"{n}_w_up"][0])
        shared[f"{n}_w_down"] = f(inputs[f"{n}_w_down"][0])
    maps = []
    for c in range(8):
        m = dict(shared)
        m["xT"] = f(np.asarray(x[c]).T)
        m["ctxT"] = f(np.asarray(inputs["ctx"][c]).T)
        cc = np.stack([cvec(inputs["c"][c]), cvec(inputs["c_ctx"])], axis=-1)
        m["cc"] = f(cc)
        maps.append(m)
    return maps


def kernel(**inputs):
    b = build()
    nc = b.finish()
    maps = make_inputs(inputs, b)
    maps = [{k: v for k, v in m.items() if k in b.ins} for m in maps]
    res = run_bass_kernel_spmd(nc, maps, core_ids=list(range(8)))
    out = np.stack([np.asarray(r["yT"]).T for r in res.results], axis=0)
    return np.ascontiguousarray(out, dtype=np.float32)
```
